# Optimizing a Trainium2 kernel written in Bass

```python
import jax, jax.numpy as jnp
from jax import lax
import numpy as np

D_MODEL = 1024
BATCH = 8
SEQ = 2048
DEPTH = 4

CTX_LEN = 256
GRID_W = 64
RET_HEADS = 8
RET_DK = 64
RET_DV = 128
RET_QK = RET_HEADS * RET_DK
RET_V = RET_HEADS * RET_DV
RET_CHUNK = 128
ROPE_BASE = 10000.0
LRU_W = D_MODEL
LRU_BLOCKS = 16
LRU_BW = LRU_W // LRU_BLOCKS
LRU_C = 8.0
CONV_W = 4
CONV_LEFT = 2
FFN_HIDDEN = 2816
FFN_RES = 0.5
N_MOD = 9
EPS = 1e-6
PROJ_SIZES = (RET_QK, RET_QK, RET_V, RET_V, LRU_W, LRU_W, D_MODEL, D_MODEL)
PROJ_W = 2 * RET_QK + 2 * RET_V + 2 * LRU_W + 2 * D_MODEL

kernel_name = "hybrid_retention_rglru_prefix_dit"


def rms_norm(x, g):
    xf = x.astype(jnp.float32)
    y = xf * lax.rsqrt(jnp.mean(xf * xf, axis=-1, keepdims=True) + EPS)
    return (y * g.astype(jnp.float32)).astype(x.dtype)


def ada_norm(x, g, shift, scale):
    return rms_norm(x, g) * (1 + scale) + shift


def ffn_sublayer(x, mod, g, w_gu, w_down):
    shift, scale, gate = mod
    h = ada_norm(x, g, shift, scale)
    u, v = jnp.split(h @ w_gu, 2, axis=-1)
    return x + FFN_RES * gate * ((jax.nn.silu(u) * v) @ w_down)


def split_proj(z):
    out, start = [], 0
    for size in PROJ_SIZES:
        out.append(z[..., start:start + size])
        start += size
    return out


def grid_rotary(rows):
    row = jnp.repeat(jnp.arange(rows, dtype=jnp.float32), GRID_W)
    col = jnp.tile(jnp.arange(GRID_W, dtype=jnp.float32), rows)
    n_f = RET_DK // 4
    inv = ROPE_BASE ** (-jnp.arange(n_f, dtype=jnp.float32) / n_f)
    ang = jnp.concatenate([row[:, None] * inv, col[:, None] * inv], axis=-1)
    return jnp.cos(ang), jnp.sin(ang)


def apply_rotary(a, cos, sin):
    a1, a2 = jnp.split(a, 2, axis=-1)
    cs, sn = cos[None, :, None, :], sin[None, :, None, :]
    return jnp.concatenate([a1 * cs - a2 * sn, a1 * sn + a2 * cs], axis=-1)


def retention_heads(z, cos, sin):
    bsz, t = z[0].shape[:2]
    q = z[0].reshape(bsz, t, RET_HEADS, RET_DK).astype(jnp.float32)
    k = z[1].reshape(bsz, t, RET_HEADS, RET_DK).astype(jnp.float32)
    v = z[2].reshape(bsz, t, RET_HEADS, RET_DV).astype(jnp.float32)
    if cos is not None:
        q = apply_rotary(q, cos, sin)
        k = apply_rotary(k, cos, sin)
    return q, k * (RET_DK ** -0.5), v


def retention_chunkwise(q, k, v, log_g, s0, include_diag):
    bsz, t, nh, _ = q.shape
    n = t // RET_CHUNK

    def to_chunks(a):
        return a.reshape(bsz, n, RET_CHUNK, nh, a.shape[-1]).transpose(1, 0, 3, 2, 4)

    idx = jnp.arange(RET_CHUNK, dtype=jnp.float32)
    rel = idx[:, None] - idx[None, :]
    mask = (rel >= 0) if include_diag else (rel > 0)
    lg = log_g[:, None, None]
    d_intra = jnp.where(mask[None], jnp.exp(lg * jnp.maximum(rel, 0.0)[None]), 0.0)
    q_dec = jnp.exp(log_g[:, None] * (idx + 1.0))[None, :, :, None]
    k_dec = jnp.exp(log_g[:, None] * (RET_CHUNK - 1.0 - idx))[None, :, :, None]
    c_dec = jnp.exp(log_g * RET_CHUNK)[None, :, None, None]

    def step(s, blk):
        qc, kc, vc = blk
        scores = jnp.einsum('bhid,bhjd->bhij', qc, kc) * d_intra
        o = (jnp.einsum('bhij,bhjv->bhiv', scores, vc)
             + jnp.einsum('bhid,bhdv->bhiv', qc * q_dec, s))
        s = s * c_dec + jnp.einsum('bhjd,bhjv->bhdv', kc * k_dec, vc)
        return s, o

    s, o = lax.scan(step, s0, (to_chunks(q), to_chunks(k), to_chunks(v)))
    o = o.transpose(1, 0, 3, 2, 4).reshape(bsz, t, nh, v.shape[-1])
    return o, s


def head_norm(o):
    mu = jnp.mean(o, axis=-1, keepdims=True)
    var = jnp.mean(jnp.square(o - mu), axis=-1, keepdims=True)
    return (o - mu) * lax.rsqrt(var + EPS)


def short_conv(u, w, b):
    t = u.shape[1]
    up = jnp.pad(u, ((0, 0), (CONV_LEFT, CONV_W - 1 - CONV_LEFT), (0, 0)))
    out = up[:, 0:t] * w[0]
    for j in range(1, CONV_W):
        out = out + up[:, j:j + t] * w[j]
    return out + b


def _lin_comb(l, r):
    return (l[0] * r[0], r[0] * l[1] + r[1])


def rg_lru_dir(u, wg, bg, lam, h0, reverse):
    if reverse:
        u = u[:, ::-1]
    bsz, t, w = u.shape
    g = jnp.einsum('btnk,gnkj->gbtnj', u.reshape(bsz, t, LRU_BLOCKS, LRU_BW),
                   wg.astype(jnp.float32)).reshape(2, bsz, t, w)
    g = g + bg.astype(jnp.float32)[:, None, None, :]
    r = jax.nn.sigmoid(g[0])
    i = jax.nn.sigmoid(g[1])
    log_a = -LRU_C * r * jax.nn.softplus(-lam.astype(jnp.float32))
    a = jnp.exp(log_a)
    b = jnp.sqrt(-jnp.expm1(2.0 * log_a)) * (i * u)
    a_cum, b_cum = lax.associative_scan(_lin_comb, (a, b), axis=1)
    h = a_cum * h0[:, None, :] + b_cum
    h_last = h[:, -1]
    if reverse:
        h = h[:, ::-1]
    return h, h_last


def merge_branches(z, o_ret, h_lru, w_ret_o, w_lru_o, w_out):
    g_ret, g_lru, gate_a, gate_b = z[3], z[5], z[6], z[7]
    bsz, t = o_ret.shape[:2]
    dt = g_ret.dtype
    o = head_norm(o_ret).reshape(bsz, t, RET_V).astype(dt)
    y_a = (o * jax.nn.silu(g_ret)) @ w_ret_o
    y_b = (h_lru.astype(dt) * jax.nn.gelu(g_lru)) @ w_lru_o
    return (jax.nn.sigmoid(gate_a) * y_a + jax.nn.sigmoid(gate_b) * y_b) @ w_out


def token_mixer(hc, hx, cos, sin, w_in, ret_logit, w_ret_o, conv_w, conv_b,
                gate_w, gate_b, lam, w_lru_o, w_out, with_ctx_out):
    zc = split_proj(hc @ w_in)
    zx = split_proj(hx @ w_in)
    bsz = hc.shape[0]
    log_g = jax.nn.log_sigmoid(ret_logit.astype(jnp.float32))

    qc, kc, vc = retention_heads(zc, None, None)
    qx, kx, vx = retention_heads(zx, cos, sin)
    s_init = jnp.zeros((bsz, RET_HEADS, RET_DK, RET_DV), jnp.float32)
    oc_f, s_f = retention_chunkwise(qc, kc, vc, log_g[0], s_init, True)
    oc_b, s_b = retention_chunkwise(qc[:, ::-1], kc[:, ::-1], vc[:, ::-1], log_g[1], s_init, False)
    ox_f, _ = retention_chunkwise(qx, kx, vx, log_g[0], s_f, True)
    ox_b, _ = retention_chunkwise(qx[:, ::-1], kx[:, ::-1], vx[:, ::-1], log_g[1], s_b, False)
    ox = ox_f + ox_b[:, ::-1]

    uc = short_conv(zc[4], conv_w, conv_b).astype(jnp.float32)
    ux = short_conv(zx[4], conv_w, conv_b).astype(jnp.float32)
    h0 = jnp.zeros((bsz, LRU_W), jnp.float32)
    hc_f, st_f = rg_lru_dir(uc, gate_w[0], gate_b[0], lam[0], h0, False)
    hc_b, st_b = rg_lru_dir(uc, gate_w[1], gate_b[1], lam[1], h0, True)
    hx_f, _ = rg_lru_dir(ux, gate_w[0], gate_b[0], lam[0], st_f, False)
    hx_b, _ = rg_lru_dir(ux, gate_w[1], gate_b[1], lam[1], st_b, True)

    y_x = merge_branches(zx, ox, hx_f + hx_b, w_ret_o, w_lru_o, w_out)
    y_c = None
    if with_ctx_out:
        y_c = merge_branches(zc, oc_f + oc_b[:, ::-1], hc_f + hc_b, w_ret_o, w_lru_o, w_out)
    return y_c, y_x


def setup_inputs(seed: int = 0) -> dict:
    key = jax.random.key(seed)
    ks = jax.random.split(key, 26)
    f32 = jnp.float32
    L, D, F = DEPTH, D_MODEL, FFN_HIDDEN

    def nrm(k, shape, fan_in):
        return jax.random.normal(k, shape, f32) * (fan_in ** -0.5)

    def small(k, shape, s=0.02):
        return s * jax.random.normal(k, shape, f32)

    gamma = 1.0 - 2.0 ** (-5.0 - jnp.arange(RET_HEADS, dtype=f32))
    ret_decay_logit = jnp.log(gamma / (1.0 - gamma)) + small(ks[12], (L, 2, RET_HEADS), 0.05)
    u = jax.random.uniform(ks[18], (L, 2, LRU_W), f32, 0.9, 0.999)
    a = u ** (1.0 / LRU_C)
    lru_lambda = jnp.log(a) - jnp.log1p(-a)

    return {
        "x": jax.random.normal(ks[0], (BATCH, SEQ, D), f32),
        "c": jax.random.normal(ks[1], (BATCH, D), f32),
        "ctx": jax.random.normal(ks[2], (BATCH, CTX_LEN, D), f32),
        "c_ctx": jax.random.normal(ks[3], (D,), f32),
        "w_mod": nrm(ks[4], (L, D, N_MOD * D), D),
        "b_mod": small(ks[5], (L, N_MOD * D)),
        "norm_g": 1.0 + small(ks[6], (L, 3, D)),
        "ffn1_w_gu": nrm(ks[7], (L, D, 2 * F), D),
        "ffn1_w_down": nrm(ks[8], (L, F, D), F),
        "ffn2_w_gu": nrm(ks[9], (L, D, 2 * F), D),
        "ffn2_w_down": nrm(ks[10], (L, F, D), F),
        "w_in": nrm(ks[11], (L, D, PROJ_W), D),
        "ret_decay_logit": ret_decay_logit,
        "w_ret_o": nrm(ks[13], (L, RET_V, D), RET_V),
        "lru_conv_w": nrm(ks[14], (L, CONV_W, LRU_W), CONV_W),
        "lru_conv_b": small(ks[15], (L, LRU_W)),
        "lru_gate_w": nrm(ks[16], (L, 2, 2, LRU_BLOCKS, LRU_BW, LRU_BW), LRU_BW),
        "lru_gate_b": small(ks[17], (L, 2, 2, LRU_W)),
        "lru_lambda": lru_lambda,
        "w_lru_o": nrm(ks[19], (L, LRU_W, D), LRU_W),
        "w_out": nrm(ks[20], (L, D, D), D),
        "final_g": 1.0 + small(ks[21], (D,)),
    }


def reference(x, c, ctx, c_ctx, w_mod, b_mod, norm_g, ffn1_w_gu, ffn1_w_down,
              ffn2_w_gu, ffn2_w_down, w_in, ret_decay_logit, w_ret_o, lru_conv_w,
              lru_conv_b, lru_gate_w, lru_gate_b, lru_lambda, w_lru_o, w_out, final_g):
    n_lat = x.shape[1]
    rows = n_lat // GRID_W
    cos, sin = grid_rotary(rows)
    for l in range(DEPTH):
        last = l == DEPTH - 1
        m_x = jnp.split((jax.nn.silu(c) @ w_mod[l] + b_mod[l])[:, None, :], N_MOD, axis=-1)
        m_c = jnp.split(jax.nn.silu(c_ctx) @ w_mod[l] + b_mod[l], N_MOD, axis=-1)

        ctx = ffn_sublayer(ctx, m_c[0:3], norm_g[l, 0], ffn1_w_gu[l], ffn1_w_down[l])
        x = ffn_sublayer(x, m_x[0:3], norm_g[l, 0], ffn1_w_gu[l], ffn1_w_down[l])

        hc = ada_norm(ctx, norm_g[l, 1], m_c[3], m_c[4])
        hx = ada_norm(x, norm_g[l, 1], m_x[3], m_x[4])
        y_c, y_x = token_mixer(hc, hx, cos, sin, w_in[l], ret_decay_logit[l], w_ret_o[l],
                               lru_conv_w[l], lru_conv_b[l], lru_gate_w[l], lru_gate_b[l],
                               lru_lambda[l], w_lru_o[l], w_out[l], not last)
        x = x + m_x[5] * y_x

        x = ffn_sublayer(x, m_x[6:9], norm_g[l, 2], ffn2_w_gu[l], ffn2_w_down[l])
        if not last:
            ctx = ctx + m_c[5] * y_c
            ctx = ffn_sublayer(ctx, m_c[6:9], norm_g[l, 2], ffn2_w_gu[l], ffn2_w_down[l])
    return rms_norm(x, final_g)
```

```python
import contextlib
import numpy as np
import ml_dtypes
import concourse.bass as bass
import concourse.mybir as mybir
from concourse.bass_utils import run_bass_kernel_spmd

F32 = mybir.dt.float32
BF16 = mybir.dt.bfloat16
AF = mybir.ActivationFunctionType
ALU = mybir.AluOpType

EPOCH = 50000


class _Op:
    __slots__ = ("stream", "idx", "fn", "deps", "dma", "dmasem", "dmaval", "marked", "incsem", "incval")

    def __init__(self, stream, idx, fn):
        self.stream = stream
        self.idx = idx
        self.fn = fn
        self.deps = []
        self.dma = False
        self.dmasem = None
        self.dmaval = 0
        self.marked = False
        self.incsem = None
        self.incval = 0


class Prog:
    def __init__(self, nc):
        self.nc = nc
        self.streams = {s: [] for s in ("pe", "act", "dve", "pool", "sp")}
        self.res = {}
        self.dma_cum = {}
        self.semres = {}
        self.sem_stream = {}

    def _add(self, stream, fn, reads, writes, dma=False, sem=None, extra=()):
        lst = self.streams[stream]
        op = _Op(stream, len(lst), fn)
        lst.append(op)
        deps = list(extra)
        if dma:
            op.dma = True
            op.dmasem = sem
            self.sem_stream.setdefault(sem, stream)
            self.dma_cum[sem] = self.dma_cum.get(sem, 0) + 16
            op.dmaval = self.dma_cum[sem]
            deps.extend(self.semres.get(sem, []))
            self.semres[sem] = []
            myref = ("d", sem, op.dmaval)
            rref = ("i", stream, op.idx)
        else:
            myref = ("c", stream, op.idx)
            rref = myref
        reads = list(dict.fromkeys(reads))
        writes = list(dict.fromkeys(writes))
        for r in reads:
            st = self.res.setdefault(r, [None, []])
            if st[0] is not None:
                deps.append(st[0])
            st[1].append(rref)
        for w in writes:
            st = self.res.setdefault(w, [None, []])
            if st[0] is not None:
                deps.append(st[0])
            deps.extend(st[1])
            st[0] = myref
            st[1] = []
        seen = set()
        for d in deps:
            if d[0] == "d" and not (dma and d[1] == sem):
                d = ("d", d[1], self.dma_cum[d[1]])
            if d in seen:
                continue
            seen.add(d)
            op.deps.append(d)
        if not getattr(fn, "_is_nop", False):
            for d in op.deps:
                if d[0] == "d":
                    self.semres.setdefault(d[1], []).append(rref)
        return op

    def op(self, stream, fn, reads=(), writes=()):
        return self._add(stream, fn, list(reads), list(writes))

    def dma(self, stream, fn, sem, reads=(), writes=()):
        return self._add(stream, fn, list(reads), list(writes), dma=True, sem=sem)

    def barrier(self, streams=("pe", "act", "dve", "sp")):
        refs = []
        for s in streams:
            lst = self.streams[s]
            for o in reversed(lst):
                if not o.dma and not getattr(o.fn, "_is_nop", False):
                    refs.append(("c", s, o.idx))
                    break
        for sem, st in self.sem_stream.items():
            if st in streams:
                refs.append(("d", sem, self.dma_cum[sem]))
        def _nop(e):
            return e.nop()
        _nop._is_nop = True
        for s in streams:
            self._add(s, _nop, [], [], extra=[r for r in refs if not (r[0] == "c" and r[1] == s)])

    def emit(self):
        nc = self.nc

        def norm(d):
            if d[0] == "i":
                o = self.streams[d[1]][d[2]]
                return ("d", o.dmasem, o.dmaval)
            return d

        for s, lst in self.streams.items():
            for o in lst:
                nd = []
                for d in o.deps:
                    d = norm(d)
                    if d[0] == "c":
                        if d[1] == s:
                            if s in ("pe", "sp"):
                                continue
                            if d[2] >= o.idx:
                                continue
                        self.streams[d[1]][d[2]].marked = True
                    elif d[0] == "d":
                        if o.dma and d[1] == o.dmasem and d[2] >= o.dmaval:
                            continue
                    nd.append(d)
                o.deps = nd
        nsem = {}
        for s, lst in self.streams.items():
            cnt = 0
            for o in lst:
                if o.marked:
                    assert not o.dma and not getattr(o.fn, "_is_nop", False)
                    o.incsem = (s, cnt // EPOCH)
                    o.incval = cnt % EPOCH + 1
                    cnt += 1
            nsem[s] = (cnt + EPOCH - 1) // EPOCH
        with contextlib.ExitStack() as es:
            csem = {}
            for s, n in nsem.items():
                for e in range(n):
                    csem[(s, e)] = es.enter_context(nc.semaphore(f"c_{s}_{e}"))
            dsem = {}
            for name in self.dma_cum:
                dsem[name] = es.enter_context(nc.semaphore(f"d_{name}"))
            block = es.enter_context(nc.Block())

            def run_stream(s):
                def body(eng):
                    waited = {}
                    for o in self.streams[s]:
                        for d in o.deps:
                            if d[0] == "c":
                                po = self.streams[d[1]][d[2]]
                                key = ("c",) + po.incsem
                                val = po.incval
                                h = csem[po.incsem]
                                later = [k for k in waited if k[0] == "c" and k[1] == po.incsem[0] and k[2] > po.incsem[1]]
                                if later:
                                    continue
                            else:
                                key = ("d", d[1])
                                val = d[2]
                                h = dsem[d[1]]
                            if waited.get(key, 0) >= val:
                                continue
                            waited[key] = val
                            eng.wait_ge(h, val)
                        ins = o.fn(eng)
                        if o.dma:
                            ins.then_inc(dsem[o.dmasem], 16)
                        elif o.marked:
                            ins.then_inc(csem[o.incsem], 1)
                    if s == "sp":
                        for name, tot in self.dma_cum.items():
                            eng.wait_ge(dsem[name], tot)
                return body

            block.tensor(run_stream("pe"))
            block.scalar(run_stream("act"))
            block.vector(run_stream("dve"))
            block.gpsimd(run_stream("pool"))
            block.sync(run_stream("sp"))


D = 1024
T = 2048
CTX = 256
TT = T + CTX
L = 4
FH = 2816
NF = 22
PROJ_W = 7168
OFF_Q, OFF_K, OFF_V, OFF_GR, OFF_LX, OFF_LG, OFF_GA, OFF_GB = 0, 512, 1024, 2048, 3072, 4096, 5120, 6144
EPS = 1e-6
NT = [(0, 256, 0), (256, 512, 1), (768, 512, 1), (1280, 512, 1), (1792, 512, 1)]
NCH = 18
DBGH = 0

R_BMOD, R_NG, R_CW, R_CB, R_GB, R_LAM, R_FG, R_C, R_CC = 0, 36, 48, 64, 68, 84, 92, 93, 94

SB_BASE = 16512
SB_END = 229344
NSLOT = 4
SLOT_ELEMS = 2048


def build_program(n_layers=L, dbg=None):
    nc = bass.Bass("TRN2", target_bir_lowering=False)
    P = Prog(nc)

    def dram(name, shape, dt=F32, kind="ExternalInput"):
        return nc.dram_tensor(name, list(shape), dt, kind=kind).ap()

    x_d = dram("x", [T, D])
    ctx_d = dram("ctx", [CTX, D])
    vecs_d = dram("vecs", [128, D])
    lgraw_d = dram("lgraw", [128, 64])
    identf_d = dram("identf", [128, 128])
    relf_d = dram("relf", [128, 128])
    relb_d = dram("relb", [128, 128])
    idxq_d = dram("idxq", [128, 128])
    idxk_d = dram("idxk", [128, 16])
    rotc_d = dram("rotc", [128, TT], BF16)
    rots_d = dram("rots", [128, TT], BF16)
    w_mod_d = dram("w_mod", [L, D, 9 * D])
    wgu_d = [dram("ffn1_w_gu", [L, D, 2 * FH]), None, dram("ffn2_w_gu", [L, D, 2 * FH])]
    wdn_d = [dram("ffn1_w_down", [L, FH, D]), None, dram("ffn2_w_down", [L, FH, D])]
    w_in_d = dram("w_in", [L, D, PROJ_W])
    w_ret_o_d = dram("w_ret_o", [L, D, D])
    w_lru_o_d = dram("w_lru_o", [L, D, D])
    w_out_d = dram("w_out", [L, D, D])
    gate_w_d = dram("lru_gate_w", [L, 2, 2, 16, 64, 64])
    out_d = dram("out", [T, D], kind="ExternalOutput")
    dbg_d = dram("dbgx", [128, 8 * TT], kind="ExternalOutput") if dbg else None
    dbgr_d = dram("dbgr", [128, 6 * TT], BF16, kind="ExternalOutput") if dbg else None

    cnt = [0]
    cur = [SB_BASE]

    def salloc(name, shape, dt, at=None):
        nbytes = int(np.prod(shape[1:])) * (4 if dt == F32 else 2)
        nbytes = (nbytes + 31) // 32 * 32
        if at is None:
            off = cur[0]
            cur[0] += nbytes
        else:
            off = at[0]
            at[0] += nbytes
        assert off + nbytes <= SB_END, (name, off, nbytes)
        cnt[0] += 1
        return nc.alloc_sbuf_tensor_at(f"{name}_{cnt[0]}", list(shape), dt, offset=off)

    X = salloc("X", [128, 8, TT], F32)
    vecs = salloc("vecs", [128, 8, 96], F32)
    identf = salloc("identf", [128, 128], F32)
    identb = salloc("identb", [128, 128], BF16)
    onesb = salloc("onesb", [128, 128], BF16)
    relf = salloc("relf", [128, 128], F32)
    relb = salloc("relb", [128, 128], F32)
    idxq = salloc("idxq", [128, 128], F32)
    idxk = salloc("idxk", [128, 16], F32)
    lg = salloc("lg", [128, 64], F32)
    lgsel = salloc("lgsel", [128, 8], F32)
    csel = salloc("csel", [128, 8], F32)
    kdec = salloc("kdec", [128, 16], F32)
    lsc = salloc("lsc", [128, 8, 8], F32)
    modp = salloc("modp", [128, 2, 9, 8], F32)
    gsm = salloc("gsm", [128, 2, 3, 8], F32)
    hgm = salloc("hgm", [128, 2, 3, 8], F32)
    scb = salloc("scb", [128, 8, 2], BF16)
    gw = [salloc(f"gw{i}", [128, 4, 128], BF16) for i in range(2)]
    slots = [salloc(f"slot{i}", [128, SLOT_ELEMS], BF16) for i in range(NSLOT)]
    PH0 = cur[0]

    def phase_alloc():
        return [PH0]

    es = contextlib.ExitStack()
    NPS = 6
    ps = [es.enter_context(nc.psum_tensor(f"ps{i}", [128, 512], F32)) for i in range(NPS)]
    psb = es.enter_context(nc.psum_tensor("psb", [128, 1024], BF16))
    psn = [0]

    def nextps():
        i = psn[0] % NPS
        psn[0] += 1
        return ps[i], ("ps", i)

    def mm(out, lhsT, rhs, start, stop, reads, writes):
        P.op("pe", lambda e: e.matmul(out, lhsT=lhsT, rhs=rhs, start=start, stop=stop), reads, writes)

    def act(out, in_, func, reads, writes, bias=None, scale=None):
        kw = {}
        if bias is not None:
            kw["bias"] = bias
        if scale is not None:
            kw["scale"] = scale
        P.op("act", lambda e: e.activation(out=out, in_=in_, func=func, **kw), reads, writes)

    def tt(out, in0, in1, op, reads, writes):
        P.op("dve", lambda e: e.tensor_tensor(out=out, in0=in0, in1=in1, op=op), reads, writes)

    def ts(out, in0, s1, s2, op0, op1, reads, writes):
        if op1 is None:
            P.op("dve", lambda e: e.tensor_scalar(out=out, in0=in0, scalar1=s1, scalar2=None, op0=op0), reads, writes)
        else:
            P.op("dve", lambda e: e.tensor_scalar(out=out, in0=in0, scalar1=s1, scalar2=s2, op0=op0, op1=op1), reads, writes)

    def stt(out, in0, scalar, in1, op0, op1, reads, writes):
        P.op("dve", lambda e: e.scalar_tensor_tensor(out=out, in0=in0, scalar=scalar, in1=in1, op0=op0, op1=op1), reads, writes)

    def vcopy(out, in_, reads, writes):
        P.op("dve", lambda e: e.tensor_copy(out=out, in_=in_), reads, writes)

    def recip(out, in_, reads, writes):
        P.op("dve", lambda e: e.reciprocal(out=out, in_=in_), reads, writes)

    def memset(ap, val, writes):
        P.op("dve", lambda e: e.memset(ap, val), [], writes)

    def dma_sp(out, in_, sem, reads, writes):
        P.dma("sp", lambda e: e.dma_start(out=out, in_=in_), sem, reads, writes)

    def bmid(ap2, n):
        a = ap2.ap
        return bass.AP(ap2.tensor, ap2.offset, [list(a[0]), [0, n], list(a[1])])

    wcount = [0]

    def wload(parts):
        i = wcount[0]
        wcount[0] += 1
        s = i % NSLOT
        slot = slots[s]
        for dstf, src in parts:
            dst = dstf(slot)
            P.dma("pool", lambda e, dst=dst, src=src: e.dma_start(out=dst, in_=src), f"w{s}", [], [("slot", s)])
        return slot, ("slot", s)

    def kview(slot, k, n):
        return slot[:, 0:k * n].rearrange("p (k n) -> p k n", k=k)

    Xk = lambda dc, ti: ("X", dc, ti)

    dma_sp(identf[:], identf_d, "c0", [], ["identf"])
    dma_sp(relf[:], relf_d, "c0", [], ["relf"])
    dma_sp(relb[:], relb_d, "c0", [], ["relb"])
    dma_sp(idxq[:], idxq_d, "c0", [], ["idxq"])
    dma_sp(idxk[:], idxk_d, "c0", [], ["idxk"])
    dma_sp(lg[:], lgraw_d, "c0", [], ["lg"])
    vcopy(identb[:], identf[:], ["identf"], ["identb"])
    memset(onesb[:], 1.0, ["onesb"])
    memset(gw[0][:], 0.0, [("gw", 0)])
    memset(gw[1][:], 0.0, [("gw", 1)])

    ph = phase_alloc()
    stage = [salloc(f"stage{i}", [128, D], F32, at=ph) for i in range(4)]

    def load_T(src_rows, dst_fn, si, dkeys, ncol=128):
        st = stage[si % 4]
        dma_sp(st[:], src_rows, f"st{si % 4}", [], [("stage", si % 4)])
        for half in range(2):
            pt, pk = nextps()
            for j in range(4):
                dc = half * 4 + j
                P.op("pe", lambda e, pt=pt, j=j, dc=dc, st=st: e.transpose(out=pt[:, j * 128:(j + 1) * 128], in_=st[:, dc * 128:(dc + 1) * 128], identity=identf[:]),
                     [("stage", si % 4), "identf"], [pk])
            src = pt[:, :].rearrange("p (j n) -> p j n", j=4)[:, :, 0:ncol]
            dst = dst_fn(half)
            wk = dkeys(half)
            if (si + half) % 2 == 0:
                vcopy(dst, src, [pk], wk)
            else:
                act(dst, src, AF.Identity, [pk], wk)

    load_T(vecs_d, lambda h: vecs[:, 4 * h:4 * h + 4, :], 0, lambda h: ["vecs"], ncol=96)
    for blk in range(NCH):
        src = ctx_d[blk * 128:(blk + 1) * 128, :] if blk < 2 else x_d[(blk - 2) * 128:(blk - 1) * 128, :]
        ti = 0 if blk < 2 else 1 + (blk - 2) // 4
        load_T(src, lambda h, blk=blk: X[:, 4 * h:4 * h + 4, blk * 128:(blk + 1) * 128], blk + 1,
               lambda h, ti=ti: [Xk(dc, ti) for dc in range(4 * h, 4 * h + 4)])

    act(lg[:], lg[:], AF.Exp, ["lg"], ["lg"], scale=-1.0)
    act(lg[:], lg[:], AF.Ln, ["lg"], ["lg"], bias=1.0)
    ts(lg[:], lg[:], -1.0, None, ALU.mult, None, ["lg"], ["lg"])
    act(lsc[:], vecs[:, :, R_LAM:R_LAM + 8], AF.Exp, ["vecs"], ["lsc"], scale=-1.0)
    act(lsc[:], lsc[:], AF.Ln, ["lsc"], ["lsc"], bias=1.0)
    ts(lsc[:], lsc[:], -4.0, None, ALU.mult, None, ["lsc"], ["lsc"])
    ts(vecs[:, :, R_GB:R_GB + 16], vecs[:, :, R_GB:R_GB + 16], 0.5, None, ALU.mult, None, ["vecs"], ["vecs"])
    act(scb[:, :, 0], vecs[:, :, R_CC], AF.Silu, ["vecs"], ["scb"])
    act(scb[:, :, 1], vecs[:, :, R_C], AF.Silu, ["vecs"], ["scb"])

    def dump_dbg():
        dma_sp(dbg_d, X[:].rearrange("p k t -> p (k t)"), "dbg", [Xk(dc, ti) for dc in range(8) for ti in range(5)], ["dbg"])

    def norm_mod(tiles, k, hbuf, hcol0, hkey, sq, t32, rstd):
        for ti in tiles:
            t0, n, s = NT[ti]
            pss, pk = nextps()
            for kc in range(8):
                sqb = sq[kc % 2]
                if kc % 4 == 1:
                    tt(sqb[:, :n], X[:, kc, t0:t0 + n], X[:, kc, t0:t0 + n], ALU.mult, [Xk(kc, ti)], [("sq", kc % 2)])
                else:
                    act(sqb[:, :n], X[:, kc, t0:t0 + n], AF.Square, [Xk(kc, ti)], [("sq", kc % 2)])
                mm(pss[:, :n], onesb[:], sqb[:, :n], kc == 0, kc == 7, [("sq", kc % 2), "onesb"], [pk])
            ri = ti % len(rstd)
            rs = rstd[ri]
            act(rs[:, :n], pss[:, :n], AF.Ln, [pk], [("rstd", ri)], bias=EPS, scale=1.0 / D)
            act(rs[:, :n], rs[:, :n], AF.Exp, [("rstd", ri)], [("rstd", ri)], scale=-0.5)
            for kc in range(8):
                tb = t32[kc % 2]
                stt(tb[:, :n], X[:, kc, t0:t0 + n], gsm[:, s, k, kc:kc + 1], rs[:, :n], ALU.mult, ALU.mult,
                    [Xk(kc, ti), ("rstd", ri), "gsm"], [("t32", kc % 2)])
                if kc % 4 == 3:
                    ts(hbuf[:, kc, t0 - hcol0:t0 - hcol0 + n], tb[:, :n], modp[:, s, 3 * k, kc:kc + 1], None, ALU.add, None,
                       [("t32", kc % 2), "modp"], [(hkey, kc, ti)])
                else:
                    act(hbuf[:, kc, t0 - hcol0:t0 - hcol0 + n], tb[:, :n], AF.Identity, [("t32", kc % 2), "modp"], [(hkey, kc, ti)],
                        bias=modp[:, s, 3 * k, kc:kc + 1])

    pmod = es.enter_context(nc.psum_tensor("pmod", [128, 512], F32))
    mod_state = {"l": None, "next": 36}

    def mod_begin(l):
        mod_state["l"] = l
        mod_state["next"] = 0

    def mod_step(nsl=1, mslots=None):
        l = mod_state["l"]
        for _ in range(nsl):
            sl = mod_state["next"]
            if sl >= 36:
                return
            mod_state["next"] = sl + 1
            wv = w_mod_d[l].rearrange("(k p) n -> p k n", p=128)
            if mslots is None:
                slot, sk = wload([(lambda s: kview(s, 8, 256), wv[:, :, sl * 256:(sl + 1) * 256])])
            else:
                slot, sk = mslots[sl % 2], ("mslot", sl % 2)
                src = wv[:, :, sl * 256:(sl + 1) * 256]
                dst = kview(slot, 8, 256)
                P.dma("pool", lambda e, dst=dst, src=src: e.dma_start(out=dst, in_=src), f"m{sl % 2}", [], [sk])
            sv = kview(slot, 8, 256)
            for j in range(2):
                n = sl * 2 + j
                for kc in range(8):
                    mm(pmod[:, 2 * n:2 * n + 2], sv[:, kc, j * 128:(j + 1) * 128], scb[:, kc, :], kc == 0, kc == 7, [sk, "scb"], ["pmod"])

    def mod_finish(l):
        assert mod_state["l"] == l
        mod_step(36)
        pm, pmk = pmod, "pmod"
        for s in range(2):
            src = bass.AP(pm[:, :].tensor, pm[:, s:s + 1].offset, [list(pm[:, :].ap[0]), [16, 9], [2, 8]])
            bsrc = bass.AP(vecs[:].tensor, vecs[:, 0, R_BMOD + l * 9:R_BMOD + l * 9 + 1].offset, [list(vecs[:, 0, :].ap[0]), [1, 9], [96, 8]])
            tt(modp[:, s, :, :], src, bsrc, ALU.add, [pmk, "vecs"], ["modp"])
            for k in range(3):
                ng = vecs[:, :, R_NG + l * 3 + k]
                stt(gsm[:, s, k, :], modp[:, s, 3 * k + 1, :], 1.0, ng, ALU.add, ALU.mult, ["modp", "vecs"], ["gsm"])
                ts(hgm[:, s, k, :], modp[:, s, 3 * k + 2, :], (1.0 if k == 1 else 0.5), None, ALU.mult, None, ["modp"], ["hgm"])

    def ffn(l, k, skip_ctx):
        use_m = k == 2 and mod_state["next"] < 36
        P.barrier(("pe", "act", "dve", "sp", "pool") if use_m else ("pe", "act", "dve", "sp"))
        ph = phase_alloc()
        hTs = [salloc("hTg", [128, 8, 1280], BF16, at=ph), salloc("hTg", [128, 8, 1024], BF16, at=ph)]
        hid = salloc("hid", [128, NF, 1280], BF16, at=ph)
        sq = [salloc("sq", [128, 512], BF16, at=ph) for _ in range(2)]
        t32 = [salloc("t32", [128, 512], F32, at=ph) for _ in range(2)]
        rstd = [salloc("rstd", [128, 512], F32, at=ph) for _ in range(1)]
        sg = [salloc("sg", [128, 512], BF16, at=ph) for _ in range(2)]
        mslots = [salloc("mslot", [128, SLOT_ELEMS], BF16, at=ph) for _ in range(2)] if use_m else None
        wgu = wgu_d[k][l].rearrange("(k p) n -> p k n", p=128)
        wdn = wdn_d[k][l].rearrange("(f p) d -> p f d", p=128)
        groups = [[1, 2] if skip_ctx else [0, 1, 2], [3, 4]]
        sgc = 0
        norm_mod(groups[0], k, hTs[0], NT[groups[0][0]][0], "hg", sq, t32, rstd)
        for gi, g in enumerate(groups):
            g0 = NT[g[0]][0]
            hT = hTs[gi]
            for j in range(NF // 2):
                sA, kA = wload([(lambda s: kview(s, 8, 256), wgu[:, :, j * 256:(j + 1) * 256])])
                sB, kB = wload([(lambda s: kview(s, 8, 256), wgu[:, :, FH + j * 256:FH + (j + 1) * 256])])
                vA, vB = kview(sA, 8, 256), kview(sB, 8, 256)
                for fi in range(2):
                    f = 2 * j + fi
                    for ti in g:
                        t0, n, s = NT[ti]
                        c0 = t0 - g0
                        pa, pak = nextps()
                        pb, pbk = nextps()
                        for kc in range(8):
                            mm(pa[:, :n], vA[:, kc, fi * 128:(fi + 1) * 128], hT[:, kc, c0:c0 + n], kc == 0, kc == 7, [kA, ("hg", kc, ti)], [pak])
                        for kc in range(8):
                            mm(pb[:, :n], vB[:, kc, fi * 128:(fi + 1) * 128], hT[:, kc, c0:c0 + n], kc == 0, kc == 7, [kB, ("hg", kc, ti)], [pbk])
                        sgb = sg[sgc % 2]
                        act(sgb[:, :n], pa[:, :n], AF.Silu, [pak], [("sg", sgc % 2)])
                        tt(hid[:, f, c0:c0 + n], sgb[:, :n], pb[:, :n], ALU.mult, [("sg", sgc % 2), pbk], [("hid", f, ti)])
                        sgc += 1
                if use_m:
                    mod_step(2, mslots)
            if gi + 1 < len(groups):
                g1 = groups[gi + 1]
                norm_mod(g1, k, hTs[gi + 1], NT[g1[0]][0], "hg", sq, t32, rstd)
            for dc in range(8):
                sl = []
                for half in range(2):
                    sl.append(wload([(lambda s: kview(s, 11, 128), wdn[:, half * 11:(half + 1) * 11, dc * 128:(dc + 1) * 128])]))
                for ti in g:
                    t0, n, s = NT[ti]
                    c0 = t0 - g0
                    po, pok = nextps()
                    for f in range(NF):
                        slot, sk = sl[f // 11]
                        mm(po[:, :n], kview(slot, 11, 128)[:, f % 11, :], hid[:, f, c0:c0 + n], f == 0, f == NF - 1, [sk, ("hid", f, ti)], [pok])
                    stt(X[:, dc, t0:t0 + n], po[:, :n], hgm[:, s, k, dc:dc + 1], X[:, dc, t0:t0 + n], ALU.mult, ALU.add,
                        [pok, "hgm", Xk(dc, ti)], [Xk(dc, ti)])

    def mixer(l, last, only=None):
        P.barrier()
        ph = phase_alloc()
        hT = salloc("hT", [128, 8, TT], BF16, at=ph)
        OG = salloc("OG", [128, 8, TT], BF16, at=ph)
        TMP0 = ph[0]
        win = w_in_d[l].rearrange("(k p) n -> p k n", p=128)
        all_t = [0, 1, 2, 3, 4]
        out_t = [1, 2, 3, 4] if last else all_t

        pa = [TMP0]
        sq = [salloc("sq", [128, 512], BF16, at=pa) for _ in range(2)]
        t32 = [salloc("t32", [128, 512], F32, at=pa) for _ in range(2)]
        rstd = [salloc("rstd", [128, 512], F32, at=pa) for _ in range(2)]
        norm_mod(all_t, 1, hT, 0, "h", sq, t32, rstd)
        P.barrier()

        def hreads(ti):
            return [("h", kc, ti) for kc in range(8)]

        def final_stage(wbr_d, gate_off, pa):
            wbr = wbr_d[l].rearrange("(k p) n -> p k n", p=128)
            wo = w_out_d[l]
            Md = [salloc("Md", [128, TT], BF16, at=pa) for _ in range(2)]
            sgt = [salloc("sgt", [128, 512], BF16, at=pa) for _ in range(2)]
            c = 0
            for dc in range(8):
                slot, sk = wload([(lambda s: kview(s, 8, 256)[:, :, 0:128], wbr[:, :, dc * 128:(dc + 1) * 128]),
                                  (lambda s: kview(s, 8, 256)[:, :, 128:256], win[:, :, gate_off + dc * 128:gate_off + (dc + 1) * 128])])
                sv = kview(slot, 8, 256)
                so, sok = wload([(lambda s: s[:, 0:1024], wo[dc * 128:(dc + 1) * 128, :])])
                M = Md[dc % 2]
                for ti in out_t:
                    t0, n, s = NT[ti]
                    py, pyk = nextps()
                    pg, pgk = nextps()
                    for kc in range(8):
                        mm(py[:, :n], sv[:, kc, 0:128], OG[:, kc, t0:t0 + n], kc == 0, kc == 7, [sk, ("og", kc, ti)], [pyk])
                    for kc in range(8):
                        mm(pg[:, :n], sv[:, kc, 128:256], hT[:, kc, t0:t0 + n], kc == 0, kc == 7, [sk, ("h", kc, ti)], [pgk])
                    sb = sgt[c % 2]
                    act(sb[:, :n], pg[:, :n], AF.Sigmoid, [pgk], [("sgt", c % 2)])
                    tt(M[:, t0:t0 + n], sb[:, :n], py[:, :n], ALU.mult, [("sgt", c % 2), pyk], [("Md", dc % 2, ti)])
                    c += 1
                for ti in out_t:
                    t0, n, s = NT[ti]
                    for dc2 in range(8):
                        po, pok = nextps()
                        mm(po[:, :n], so[:, dc2 * 128:(dc2 + 1) * 128], M[:, t0:t0 + n], True, True, [sok, ("Md", dc % 2, ti)], [pok])
                        stt(X[:, dc2, t0:t0 + n], po[:, :n], hgm[:, s, 1, dc2:dc2 + 1], X[:, dc2, t0:t0 + n], ALU.mult, ALU.add,
                            [pok, "hgm", Xk(dc2, ti)], [Xk(dc2, ti)])

        def lru_branch():
            pa = [TMP0]
            UB = TT + 8
            ubuf = salloc("ubuf", [128, UB], BF16, at=pa)
            ucs = [salloc("uc", [128, TT], BF16, at=pa) for _ in range(2)]
            Afull = salloc("Afull", [128, TT], F32, at=pa)
            Iu = salloc("Iu", [128, TT], BF16, at=pa)
            S1 = salloc("S1", [128, TT], BF16, at=pa)
            Tt = [salloc("Tt", [128, 512], F32, at=pa)]
            off_t1 = pa[0]
            Tt.append(salloc("Tt", [128, 512], F32, at=pa))
            Qt = salloc("Qt", [128, 512], F32, at=pa)
            gt = salloc("gt", [128, 512], BF16, at=pa)
            gts = [(gt, "gt"), (salloc("gt2", [128, 512], BF16, at=[off_t1]), ("Qb", 1))]
            UOFF = [2, 261]
            memset(ubuf[:, 0:2], 0.0, ["ubufpad"])
            memset(ubuf[:, 258:261], 0.0, ["ubufpad"])
            memset(ubuf[:, 2309:UB], 0.0, ["ubufpad"])
            ubr = [("ubuf", ti) for ti in all_t] + ["ubufpad"]

            def load_slab(c):
                return wload([(lambda s: kview(s, 8, 256)[:, :, 0:128], win[:, :, OFF_LX + c * 128:OFF_LX + (c + 1) * 128]),
                              (lambda s: kview(s, 8, 256)[:, :, 128:256], win[:, :, OFF_LG + c * 128:OFF_LG + (c + 1) * 128])])

            def load_gw(c):
                gwt = gw[c % 2]
                for blk in range(2):
                    P.dma("pool", lambda e, blk=blk, gwt=gwt, c=c: e.dma_start(
                        out=gwt[blk * 64:(blk + 1) * 64, :, blk * 64:(blk + 1) * 64],
                        in_=gate_w_d[l, :, :, 2 * c + blk].rearrange("d g k j -> k (d g) j")),
                        f"gw{c % 2}", [], [("gw", c % 2)])

            def proj_u(c, slab):
                slot, sk = slab
                sv = kview(slot, 8, 256)
                for ti in all_t:
                    t0, n, s = NT[ti]
                    pu, puk = nextps()
                    for kc in range(8):
                        mm(pu[:, :n], sv[:, kc, 0:128], hT[:, kc, t0:t0 + n], kc == 0, kc == 7, [sk, ("h", kc, ti)], [puk])
                    uo = (UOFF[0] if ti == 0 else UOFF[1] - 256) + t0
                    act(ubuf[:, uo:uo + n], pu[:, :n], AF.Identity, [puk], [("ubuf", ti)])

            def conv_ops(c, ti):
                t0, n, s = NT[ti]
                uc = ucs[c % 2]
                base = (0 if ti == 0 else 259 - 256) + t0
                cw = lambda j: vecs[:, c, R_CW + l * 4 + j:R_CW + l * 4 + j + 1]
                ops = [lambda: ts(pmod[:, :n], ubuf[:, base:base + n], cw(0), vecs[:, c, R_CB + l:R_CB + l + 1], ALU.mult, ALU.add, ubr + ["vecs"], ["pmod"])]
                for j in (1, 2):
                    ops.append(lambda j=j: stt(pmod[:, :n], ubuf[:, base + j:base + j + n], cw(j), pmod[:, :n], ALU.mult, ALU.add, ubr + ["vecs", "pmod"], ["pmod"]))
                ops.append(lambda: stt(uc[:, t0:t0 + n], ubuf[:, base + 3:base + 3 + n], cw(3), pmod[:, :n], ALU.mult, ALU.add, ubr + ["vecs", "pmod"], [("uc", c % 2, ti)]))
                return ops

            def conv_tile(c, ti):
                for f_ in conv_ops(c, ti):
                    f_()

            slabs = {0: load_slab(0)}
            load_gw(0)
            slabs[1] = load_slab(1)
            proj_u(0, slabs[0])
            for ti in all_t:
                conv_tile(0, ti)
            tk = 0
            qc = [0]
            for c in range(8):
                slot, sk = slabs[c]
                sv = kview(slot, 8, 256)
                gwt = gw[c % 2]
                uc = ucs[c % 2]
                hfc = OG[:, c, :]
                if c + 1 < 8:
                    proj_u(c + 1, slabs[c + 1])
                Ak = lambda ti: ("A", ti)
                Ik = lambda ti: ("Iu", ti)
                Sk = lambda ti: ("S1", ti)
                allA = [Ak(ti) for ti in all_t]
                allI = [Ik(ti) for ti in all_t]
                allS = [Sk(ti) for ti in all_t]
                ogk = [("og", c, ti) for ti in all_t]
                cq = []
                if c + 1 < 8:
                    for ti in all_t:
                        cq += conv_ops(c + 1, ti)
                for d in range(2):
                    rb = R_GB + l * 4 + d * 2
                    hl = lsc[:, c, l * 2 + d:l * 2 + d + 1]
                    for ti in ((0, 4, 3, 2, 1) if d == 0 else all_t):
                        t0, n, s = NT[ti]
                        pr, prk = nextps()
                        pi, pik = nextps()
                        mm(pr[:, :n], gwt[:, d * 2 + 0, :], uc[:, t0:t0 + n], True, True, [("gw", c % 2), ("uc", c % 2, ti)], [prk])
                        mm(pi[:, :n], gwt[:, d * 2 + 1, :], uc[:, t0:t0 + n], True, True, [("gw", c % 2), ("uc", c % 2, ti)], [pik])
                        act(pr[:, :n], pr[:, :n], AF.Tanh, [prk, "vecs"], [prk], bias=vecs[:, c, rb:rb + 1], scale=0.5)
                        if d == 1:
                            act(hfc[:, t0:t0 + n], Afull[:, t0:t0 + n], AF.Identity, [Ak(ti)], [("og", c, ti)])
                        act(Afull[:, t0:t0 + n], pr[:, :n], AF.Exp, [prk, "lsc"], [Ak(ti)], bias=hl, scale=hl)
                        act(pi[:, :n], pi[:, :n], AF.Tanh, [pik, "vecs"], [pik], bias=vecs[:, c, rb + 1:rb + 2], scale=0.5)
                        stt(Iu[:, t0:t0 + n], pi[:, :n], 1.0, uc[:, t0:t0 + n], ALU.add, ALU.mult, [pik, ("uc", c % 2, ti)], [Ik(ti)])
                        qb = [Qt, Tt[1]][qc[0] % 2]
                        qk = ("Qb", qc[0] % 2)
                        qc[0] += 1
                        P.op("pool", lambda e, t0=t0, n=n, qb=qb: e.tensor_tensor(out=qb[:, :n], in0=Afull[:, t0:t0 + n], in1=Afull[:, t0:t0 + n], op=ALU.mult),
                             [Ak(ti)], [qk])
                        P.op("pool", lambda e, t0=t0, n=n, qb=qb: e.tensor_scalar(out=S1[:, t0:t0 + n], in0=qb[:, :n], scalar1=-1.0, scalar2=1.0, op0=ALU.mult, op1=ALU.add),
                             [qk], [Sk(ti)])
                        for _ in range(2):
                            if cq:
                                cq.pop(0)()
                    if d == 0:
                        if c + 2 < 8:
                            slabs[c + 2] = load_slab(c + 2)
                        if c + 1 < 8:
                            load_gw(c + 1)
                    act(S1[:, :], S1[:, :], AF.Sqrt, allS, allS, scale=0.25)
                    tt(Iu[:, :], Iu[:, :], S1[:, :], ALU.mult, allI + allS, allI)
                    if d == 0:
                        for ti in all_t:
                            t0, n, s = NT[ti]
                            init = 0.0 if ti == 0 else Afull[:, t0 - 1:t0]
                            rd = [Ak(ti), Ik(ti)] + ([Ak(ti - 1)] if ti else [])
                            P.op("dve", lambda e, t0=t0, n=n, init=init: e.tensor_tensor_scan(out=Afull[:, t0:t0 + n], data0=Afull[:, t0:t0 + n], data1=Iu[:, t0:t0 + n],
                                                                                           initial=init, op0=ALU.mult, op1=ALU.add), rd, [Ak(ti)])
                    else:
                        def out_tile(ti, oi):
                            t0, n, s = NT[ti]
                            gb, gk = gts[oi % 2]
                            pg, pgk = nextps()
                            for kc in range(8):
                                mm(pg[:, :n], sv[:, kc, 128:256], hT[:, kc, t0:t0 + n], kc == 0, kc == 7, [sk, ("h", kc, ti)], [pgk])
                            act(gb[:, :n], pg[:, :n], AF.Gelu_apprx_tanh, [pgk], [gk])
                            tt(Tt[0][:, :n], hfc[:, t0:t0 + n], Afull[:, t0:t0 + n], ALU.add, [("og", c, ti), Ak(ti)], [("Tt", 0)])
                            tt(OG[:, c, t0:t0 + n], Tt[0][:, :n], gb[:, :n], ALU.mult, [("Tt", 0), gk], [("og", c, ti)])

                        while cq:
                            cq.pop(0)()
                        prev = None
                        order = (0, 4, 3, 2, 1)
                        for oi, ti in enumerate(order):
                            t0, n, s = NT[ti]
                            if prev is None:
                                init = 0.0
                            else:
                                p0 = NT[prev][0]
                                init = Afull[:, p0:p0 + 1]
                            rd = [Ak(ti), Ik(ti)] + ([Ak(prev)] if prev is not None else [])
                            P.op("dve", lambda e, t0=t0, n=n, init=init: e.tensor_tensor_scan(out=Afull[:, t0:t0 + n][:, ::-1], data0=Afull[:, t0:t0 + n][:, ::-1],
                                                                                           data1=Iu[:, t0:t0 + n][:, ::-1], initial=init, op0=ALU.mult, op1=ALU.add),
                                 rd, [Ak(ti)])
                            if prev is not None:
                                out_tile(prev, oi - 1)
                            prev = ti
                        out_tile(prev, len(order) - 1)
            P.barrier()
            pa = [TMP0]
            final_stage(w_lru_o_d, OFF_GB, pa)
            P.barrier()

        def ret_branch():
            pa = [TMP0]
            qd = salloc("qd", [128, TT], BF16, at=pa)
            kT = salloc("kT", [128, TT], BF16, at=pa)
            vtok = salloc("vtok", [128, NCH, 128], BF16, at=pa)
            kd = salloc("kd", [128, NCH, 128], BF16, at=pa)
            Scat = salloc("Scat", [128, NCH, 128], BF16, at=pa)
            rc = [salloc("rc", [128, 512], BF16, at=pa) for _ in range(2)]
            rs_ = [salloc("rs", [128, 512], BF16, at=pa) for _ in range(2)]
            t1 = salloc("t1", [128, 512], F32, at=pa)
            t2 = salloc("t2", [128, 512], F32, at=pa)
            DT = salloc("DT", [128, 128], F32, at=pa)
            dtab = salloc("dtab", [128, 128], BF16, at=pa)
            PT = [salloc("PT", [128, 512], BF16, at=pa) for _ in range(2)]
            qdec = [salloc("qdec", [128, 512], BF16, at=pa) for _ in range(2)]
            Sst = [salloc("Sst", [128, 128], F32, at=pa) for _ in range(2)]
            ob, o2, sgr = rc[0], rs_[0], rc[1]
            OBK, O2K, SGK = ("rc", 0), ("rs", 0), ("rc", 1)
            mean, var = t1, t2
            cen = salloc("cen", [128, 512], F32, at=pa)

            vcopy(lgsel[0:64, :], lg[0:64, l * 16:l * 16 + 8], ["lg"], ["lgsel"])
            vcopy(lgsel[64:128, :], lg[64:128, l * 16 + 8:l * 16 + 16], ["lg"], ["lgsel"])
            act(csel[:], lgsel[:], AF.Exp, ["lgsel"], ["csel"], scale=128.0)
            tt(kdec[:], lg[:, l * 16:(l + 1) * 16], idxk[:], ALU.mult, ["lg", "idxk"], ["kdec"])
            act(kdec[:], kdec[:], AF.Exp, ["kdec"], ["kdec"])

            rcount = [0]

            def proj_rot(slot_v, sk, ti, dst, dkey, kscale):
                t0, n, s = NT[ti]
                ri = rcount[0] % 2
                rcount[0] += 1
                dma_sp(rc[ri][:, :n], rotc_d[:, t0:t0 + n], f"rc{ri}", [], [("rc", ri)])
                dma_sp(rs_[ri][:, :n], rots_d[:, t0:t0 + n], f"rs{ri}", [], [("rs", ri)])
                p1, p1k = nextps()
                p2, p2k = nextps()
                for kc in range(8):
                    mm(p1[:, :n], slot_v[:, kc, 0:128], hT[:, kc, t0:t0 + n], kc == 0, kc == 7, [sk, ("h", kc, ti)], [p1k])
                for kc in range(8):
                    mm(p2[:, :n], slot_v[:, kc, 128:256], hT[:, kc, t0:t0 + n], kc == 0, kc == 7, [sk, ("h", kc, ti)], [p2k])
                stt(t1[:, :n], p1[:, :n], kscale, rc[ri][:, :n], ALU.mult, ALU.mult, [p1k, ("rc", ri)], ["t1"])
                stt(t2[:, :n], p2[:, :n], kscale, rs_[ri][:, :n], ALU.mult, ALU.mult, [p2k, ("rs", ri)], ["t2"])
                tt(dst[:, t0:t0 + n], t1[:, :n], t2[:, :n], ALU.add, ["t1", "t2"], [(dkey, ti)])

            def dup_cols(s, off, c0, w):
                return [(lambda sl, o=off + r * w: kview(sl, 8, 256)[:, :, o:o + w], win[:, :, c0:c0 + w]) for r in range(2)]

            for h in range(8):
                r0 = 0 if h % 2 == 0 else 64
                qc = OFF_Q + h * 64
                parts = []
                parts += [(lambda sl, o=r * 64: kview(sl, 8, 256)[:, :, o:o + 64], win[:, :, qc:qc + 64]) for r in range(2)]
                for r in range(2):
                    parts.append((lambda sl, o=128 + r * 64: kview(sl, 8, 256)[:, :, o:o + 32], win[:, :, qc + 32:qc + 64]))
                    parts.append((lambda sl, o=128 + r * 64 + 32: kview(sl, 8, 256)[:, :, o:o + 32], win[:, :, qc:qc + 32]))
                sq_, sqk = wload(parts)
                if h % 2 == 0:
                    kc_ = OFF_K + h * 64
                    parts = [(lambda sl: kview(sl, 8, 256)[:, :, 0:128], win[:, :, kc_:kc_ + 128])]
                    for r in range(2):
                        parts.append((lambda sl, o=128 + r * 64: kview(sl, 8, 256)[:, :, o:o + 32], win[:, :, kc_ + r * 64 + 32:kc_ + r * 64 + 64]))
                        parts.append((lambda sl, o=128 + r * 64 + 32: kview(sl, 8, 256)[:, :, o:o + 32], win[:, :, kc_ + r * 64:kc_ + r * 64 + 32]))
                    sk_, skk = wload(parts)
                sv_, svk = wload([(lambda sl: kview(sl, 8, 256)[:, :, 0:128], win[:, :, OFF_V + h * 128:OFF_V + (h + 1) * 128]),
                                  (lambda sl: kview(sl, 8, 256)[:, :, 128:256], win[:, :, OFF_GR + h * 128:OFF_GR + (h + 1) * 128])])
                svv = kview(sv_, 8, 256)
                for ti in all_t:
                    proj_rot(kview(sq_, 8, 256), sqk, ti, qd, "qd", 1.0)
                    if h % 2 == 0:
                        proj_rot(kview(sk_, 8, 256), skk, ti, kT, "kT", 0.125)
                for ti in all_t:
                    t0, n, s = NT[ti]
                    pv, pvk = nextps()
                    nch = n // 128
                    for cc in range(nch):
                        ch = t0 // 128 + cc
                        for kc in range(8):
                            mm(pv[:, cc * 128:(cc + 1) * 128], hT[:, kc, ch * 128:(ch + 1) * 128], svv[:, kc, 0:128], kc == 0, kc == 7,
                               [("h", kc, ti), svk], [pvk])
                    act(vtok[:, t0 // 128:t0 // 128 + nch, :], pv[:, :n].rearrange("p (c v) -> p c v", c=nch), AF.Identity, [pvk], [("vtok", ti)])
                for (c0, c1) in ((0, 8), (8, 16), (16, 18)):
                    tis = sorted(set((0 if ch < 2 else 1 + (ch - 2) // 4) for ch in range(c0, c1)))
                    for ch in range(c0, c1):
                        P.op("pe", lambda e, ch=ch, c0=c0: e.transpose(out=psb[:, (ch - c0) * 128:(ch - c0 + 1) * 128], in_=kT[:, ch * 128:(ch + 1) * 128],
                                                                       identity=identb[:]),
                             [("kT", ti) for ti in tis] + ["identb"], ["psb"])
                    src = psb[:, 0:(c1 - c0) * 128].rearrange("p (c d) -> p c d", d=128)[:, :, r0:r0 + 64]
                    ts(kd[:, c0:c1, 0:64], src, kdec[:, h:h + 1], None, ALU.mult, None, ["psb", "kdec"], [("kd", c0)])
                    ts(kd[:, c0:c1, 64:128], src, kdec[:, 8 + h:9 + h], None, ALU.mult, None, ["psb", "kdec"], [("kd", c0)])
                kdr = [("kd", 0), ("kd", 8), ("kd", 16)]
                vr = [("vtok", ti) for ti in all_t]
                ts(DT[:], relf[:], lg[:, l * 16 + h:l * 16 + h + 1], None, ALU.mult, None, ["relf", "lg"], ["DT"])
                stt(DT[:], relb[:], lg[:, l * 16 + 8 + h:l * 16 + 9 + h], DT[:], ALU.mult, ALU.add, ["relb", "lg", "DT"], ["DT"])
                act(DT[:], DT[:], AF.Exp, ["DT"], ["DT"])
                act(dtab[:], idxq[:], AF.Exp, ["idxq", "lgsel"], ["dtab"], scale=lgsel[:, h:h + 1])
                kvp = []
                for b in range(5):
                    pk_, pkk = nextps()
                    kvp.append((pk_, pkk))
                for ch in range(NCH):
                    pk_, pkk = kvp[ch // 4]
                    mm(pk_[:, (ch % 4) * 128:(ch % 4 + 1) * 128], kd[:, ch, :], vtok[:, ch, :], True, True, kdr + vr, [pkk])

                def kvs(rows, ch):
                    pk_, pkk = kvp[ch // 4]
                    return pk_[rows, (ch % 4) * 128:(ch % 4 + 1) * 128], pkk

                fr = slice(0, 64)
                br = slice(64, 128)
                memset(Sst[0][:, :], 0.0, [("Sf", 0), ("Sb", 0)])
                memset(Scat[fr, 0, :], 0.0, [("ScatF", 0)])
                memset(Scat[br, 1, :], 0.0, [("ScatB", 1)])
                seq = [(1, 0), (0, 17)] + [(n_, n_ - 1) for n_ in range(17, 2, -1)]
                for j in range(NCH - 1):
                    cur, nx = j % 2, (j + 1) % 2
                    kv, kvk = kvs(fr, j)
                    stt(Sst[nx][fr, :], Sst[cur][fr, :], csel[fr, h:h + 1], kv, ALU.mult, ALU.add, [("Sf", cur), "csel", kvk], [("Sf", nx)])
                    src_ch, dst_ch = seq[j]
                    kv, kvk = kvs(br, src_ch)
                    stt(Sst[nx][br, :], Sst[cur][br, :], csel[br, h:h + 1], kv, ALU.mult, ALU.add, [("Sb", cur), "csel", kvk], [("Sb", nx)])
                    act(Scat[fr, j + 1, :], Sst[nx][fr, :], AF.Identity, [("Sf", nx)], [("ScatF", j + 1)])
                    act(Scat[br, dst_ch, :], Sst[nx][br, :], AF.Identity, [("Sb", nx)], [("ScatB", dst_ch)])
                for ti in all_t:
                    t0, n, s = NT[ti]
                    nch = n // 128
                    cb = t0 // 128
                    pi_ = ti % 2
                    psx, psk = nextps()
                    for cc in range(nch):
                        ch = cb + cc
                        mm(psx[:, cc * 128:(cc + 1) * 128], kT[r0:r0 + 64, ch * 128:(ch + 1) * 128], qd[r0:r0 + 64, ch * 128:(ch + 1) * 128], True, True,
                           [("kT", ti), ("qd", ti)], [psk])
                    tt(PT[pi_][:, :n].rearrange("p (c i) -> p c i", c=nch), psx[:, :n].rearrange("p (c i) -> p c i", c=nch), bmid(DT[:, :], nch), ALU.mult,
                       [psk, "DT"], [("PT", pi_)])
                    tt(qdec[pi_][:, :n].rearrange("p (c i) -> p c i", c=nch), qd[:, t0:t0 + n].rearrange("p (c i) -> p c i", c=nch), bmid(dtab[:, :], nch), ALU.mult,
                       [("qd", ti), "dtab"], [("qdec", pi_)])
                    po, pok = nextps()
                    for cc in range(nch):
                        ch = cb + cc
                        mm(po[:, cc * 128:(cc + 1) * 128], vtok[:, ch, :], PT[pi_][:, cc * 128:(cc + 1) * 128], True, False, vr + [("PT", pi_)], [pok])
                        mm(po[:, cc * 128:(cc + 1) * 128], Scat[:, ch, :], qdec[pi_][:, cc * 128:(cc + 1) * 128], False, True,
                           [("ScatF", ch), ("ScatB", ch), ("qdec", pi_)], [pok])
                    act(ob[:, :n], po[:, :n], AF.Identity, [pok], [OBK])
                    act(o2[:, :n], po[:, :n], AF.Square, [pok], [O2K])
                    pm1, pm1k = nextps()
                    pm2, pm2k = nextps()
                    mm(pm1[:, :n], onesb[:], ob[:, :n], True, True, [OBK, "onesb"], [pm1k])
                    mm(pm2[:, :n], onesb[:], o2[:, :n], True, True, [O2K, "onesb"], [pm2k])
                    ts(mean[:, :n], pm1[:, :n], 1.0 / 128, None, ALU.mult, None, [pm1k], ["t1"])
                    tt(var[:, :n], mean[:, :n], mean[:, :n], ALU.mult, ["t1"], ["t2"])
                    stt(var[:, :n], pm2[:, :n], 1.0 / 128, var[:, :n], ALU.mult, ALU.subtract, [pm2k, "t2"], ["t2"])
                    ts(var[:, :n], var[:, :n], 0.0, EPS, ALU.max, ALU.add, ["t2"], ["t2"])
                    act(var[:, :n], var[:, :n], AF.Ln, ["t2"], ["t2"])
                    act(var[:, :n], var[:, :n], AF.Exp, ["t2"], ["t2"], scale=-0.5)
                    tt(cen[:, :n], po[:, :n], mean[:, :n], ALU.subtract, [pok, "t1"], ["cen"])
                    tt(cen[:, :n], cen[:, :n], var[:, :n], ALU.mult, ["cen", "t2"], ["cen"])
                    pg, pgk = nextps()
                    for kc in range(8):
                        mm(pg[:, :n], svv[:, kc, 128:256], hT[:, kc, t0:t0 + n], kc == 0, kc == 7, [svk, ("h", kc, ti)], [pgk])
                    act(sgr[:, :n], pg[:, :n], AF.Silu, [pgk], [SGK])
                    tt(OG[:, h, t0:t0 + n], cen[:, :n], sgr[:, :n], ALU.mult, ["cen", SGK], [("og", h, ti)])
                if dbg and h == DBGH:
                    P.barrier()
                    for i_, tsr in enumerate((qd[:, :], kT[:, :], vtok[:].rearrange("p c v -> p (c v)"), kd[:].rearrange("p c v -> p (c v)"),
                                              Scat[:].rearrange("p c v -> p (c v)"), OG[:, h, :])):
                        dma_sp(dbgr_d[:, i_ * TT:(i_ + 1) * TT], tsr, "dbgr", [], [("dbgr", i_)])
                    P.barrier()
            P.barrier()
            pa = [TMP0]
            final_stage(w_ret_o_d, OFF_GA, pa)

        if only != 'ret':
            lru_branch()
        if only != 'lru':
            ret_branch()

    def final_out():
        P.barrier()
        ph = phase_alloc()
        sq = [salloc("sq", [128, 512], BF16, at=ph) for _ in range(2)]
        rstd = [salloc("rstd", [128, 512], F32, at=ph) for _ in range(2)]
        y = [salloc("y", [128, 8, 512], F32, at=ph) for _ in range(2)]
        ost = [salloc("ost", [128, D], F32, at=ph) for _ in range(2)]
        oc = 0
        for ti in (1, 2, 3, 4):
            t0, n, s = NT[ti]
            pss, pk = nextps()
            for kc in range(8):
                sqb = sq[kc % 2]
                act(sqb[:, :n], X[:, kc, t0:t0 + n], AF.Square, [Xk(kc, ti)], [("sq", kc % 2)])
                mm(pss[:, :n], onesb[:], sqb[:, :n], kc == 0, kc == 7, [("sq", kc % 2), "onesb"], [pk])
            rs = rstd[ti % 2]
            act(rs[:, :n], pss[:, :n], AF.Ln, [pk], [("rstd", ti % 2)], bias=EPS, scale=1.0 / D)
            act(rs[:, :n], rs[:, :n], AF.Exp, [("rstd", ti % 2)], [("rstd", ti % 2)], scale=-0.5)
            yb = y[ti % 2]
            for kc in range(8):
                stt(yb[:, kc, :n], X[:, kc, t0:t0 + n], vecs[:, kc, R_FG:R_FG + 1], rs[:, :n], ALU.mult, ALU.mult,
                    [Xk(kc, ti), ("rstd", ti % 2), "vecs"], [("y", ti % 2, kc)])
            for cc in range(n // 128):
                ob_ = ost[oc % 2]
                for half in range(2):
                    pt, ptk = nextps()
                    for j in range(4):
                        kc = half * 4 + j
                        P.op("pe", lambda e, pt=pt, j=j, kc=kc, yb=yb, cc=cc: e.transpose(out=pt[:, j * 128:(j + 1) * 128], in_=yb[:, kc, cc * 128:(cc + 1) * 128], identity=identf[:]),
                             [("y", ti % 2, kc), "identf"], [ptk])
                    if half == 0:
                        vcopy(ob_[:, 0:512], pt[:, :], [ptk], [("ost", oc % 2)])
                    else:
                        act(ob_[:, 512:1024], pt[:, :], AF.Identity, [ptk], [("ost", oc % 2)])
                row0 = t0 - CTX + cc * 128
                dma_sp(out_d[row0:row0 + 128, :], ob_[:], f"o{oc % 2}", [("ost", oc % 2)], [("out", oc)])
                oc += 1

    done = False
    for l in range(n_layers):
        last = l == L - 1
        if l == 0:
            mod_begin(0)
        mod_finish(l)
        ffn(l, 0, False)
        if dbg == (l, "ffn1"):
            done = True
            break
        if dbg and dbg[0] == l and dbg[1] in ("mixer_lru", "mixer_ret"):
            mixer(l, last, only=dbg[1][6:])
            done = True
            break
        mixer(l, last)
        if dbg == (l, "mixer"):
            done = True
            break
        if l + 1 < n_layers:
            mod_begin(l + 1)
        ffn(l, 2, last)
        if dbg == (l, "ffn2"):
            done = True
            break
    if dbg:
        P.barrier()
        dump_dbg()
    final_out()
    P.emit()
    es.close()
    return nc


def _consts():
    idx = np.arange(128, dtype=np.float32)
    rel = idx[None, :] - idx[:, None]
    relf = np.maximum(rel, 0.0).astype(np.float32)
    relb = np.maximum(-rel, 0.0).astype(np.float32)
    idxq = np.empty((128, 128), np.float32)
    idxq[:64, :] = idx[None, :] + 1.0
    idxq[64:, :] = 128.0 - idx[None, :]
    idxk = np.empty((128, 16), np.float32)
    idxk[:, :8] = (127.0 - idx)[:, None]
    idxk[:, 8:] = idx[:, None]
    rows = T // 64
    row = np.repeat(np.arange(rows, dtype=np.float32), 64)
    col = np.tile(np.arange(64, dtype=np.float32), rows)
    n_f = 16
    inv = (10000.0 ** (-np.arange(n_f, dtype=np.float32) / n_f)).astype(np.float32)
    ang = np.concatenate([row[:, None] * inv, col[:, None] * inv], axis=-1)
    cos, sin = np.cos(ang).astype(np.float32), np.sin(ang).astype(np.float32)
    C64 = np.concatenate([cos, cos], axis=1).T
    S64 = np.concatenate([-sin, sin], axis=1).T
    rotc = np.ones((128, TT), np.float32)
    rots = np.zeros((128, TT), np.float32)
    rotc[:64, CTX:] = C64
    rotc[64:, CTX:] = C64
    rots[:64, CTX:] = S64
    rots[64:, CTX:] = S64
    return dict(identf=np.eye(128, dtype=np.float32), relf=relf, relb=relb, idxq=idxq, idxk=idxk,
                rotc=rotc.astype(ml_dtypes.bfloat16), rots=rots.astype(ml_dtypes.bfloat16))


_NC_CACHE = {}


def _prep_inputs(inp):
    f = lambda a: np.ascontiguousarray(np.asarray(a, dtype=np.float32))
    consts = _consts()
    shared = {k: f(inp[k]) for k in ("w_mod", "ffn1_w_gu", "ffn1_w_down", "ffn2_w_gu", "ffn2_w_down", "w_in",
                                      "w_ret_o", "w_lru_o", "w_out", "lru_gate_w")}
    shared.update(consts)
    shared["lgraw"] = np.ascontiguousarray(np.broadcast_to(f(inp["ret_decay_logit"]).reshape(1, 64), (128, 64)))
    base = np.zeros((128, D), np.float32)
    base[R_BMOD:R_BMOD + 36] = f(inp["b_mod"]).reshape(36, D)
    base[R_NG:R_NG + 12] = f(inp["norm_g"]).reshape(12, D)
    base[R_CW:R_CW + 16] = f(inp["lru_conv_w"]).reshape(16, D)
    base[R_CB:R_CB + 4] = f(inp["lru_conv_b"]).reshape(4, D)
    base[R_GB:R_GB + 16] = f(inp["lru_gate_b"]).reshape(16, D)
    base[R_LAM:R_LAM + 8] = f(inp["lru_lambda"]).reshape(8, D)
    base[R_FG] = f(inp["final_g"])
    base[R_CC] = f(inp["c_ctx"])
    x = f(inp["x"])
    ctx = f(inp["ctx"])
    c = f(inp["c"])
    maps = []
    for b in range(8):
        v = base.copy()
        v[R_C] = c[b]
        m = dict(shared)
        m["x"] = x[b]
        m["ctx"] = ctx[b]
        m["vecs"] = v
        maps.append(m)
    return maps


def kernel(**inputs):
    if "nc" not in _NC_CACHE:
        _NC_CACHE["nc"] = build_program()
    nc = _NC_CACHE["nc"]
    maps = _prep_inputs(inputs)
    res = run_bass_kernel_spmd(nc, maps, core_ids=list(range(8)))
    return np.stack([np.asarray(r["out"], dtype=np.float32) for r in res.results], axis=0)
```

```python
import contextlib
import numpy as np
import ml_dtypes
import concourse.bass as bass
import concourse.mybir as mybir
from concourse.bass_utils import run_bass_kernel_spmd

F32 = mybir.dt.float32
BF16 = mybir.dt.bfloat16
AF = mybir.ActivationFunctionType
ALU = mybir.AluOpType

EPOCH = 50000


class _Op:
    __slots__ = ("stream", "idx", "fn", "deps", "dma", "dmasem", "dmaval", "marked", "incsem", "incval")

    def __init__(self, stream, idx, fn):
        self.stream = stream
        self.idx = idx
        self.fn = fn
        self.deps = []
        self.dma = False
        self.dmasem = None
        self.dmaval = 0
        self.marked = False
        self.incsem = None
        self.incval = 0


class Prog:
    def __init__(self, nc):
        self.nc = nc
        self.streams = {s: [] for s in ("pe", "act", "dve", "pool", "sp")}
        self.res = {}
        self.dma_cum = {}
        self.semres = {}
        self.sem_stream = {}

    def _add(self, stream, fn, reads, writes, dma=False, sem=None, extra=()):
        lst = self.streams[stream]
        op = _Op(stream, len(lst), fn)
        lst.append(op)
        deps = list(extra)
        if dma:
            op.dma = True
            op.dmasem = sem
            self.sem_stream.setdefault(sem, stream)
            self.dma_cum[sem] = self.dma_cum.get(sem, 0) + 16
            op.dmaval = self.dma_cum[sem]
            deps.extend(self.semres.get(sem, []))
            self.semres[sem] = []
            myref = ("d", sem, op.dmaval)
            rref = ("i", stream, op.idx)
        else:
            myref = ("c", stream, op.idx)
            rref = myref
        reads = list(dict.fromkeys(reads))
        writes = list(dict.fromkeys(writes))
        for r in reads:
            st = self.res.setdefault(r, [None, []])
            if st[0] is not None:
                deps.append(st[0])
            st[1].append(rref)
        for w in writes:
            st = self.res.setdefault(w, [None, []])
            if st[0] is not None:
                deps.append(st[0])
            deps.extend(st[1])
            st[0] = myref
            st[1] = []
        seen = set()
        for d in deps:
            if d[0] == "d" and not (dma and d[1] == sem):
                d = ("d", d[1], self.dma_cum[d[1]])
            if d in seen:
                continue
            seen.add(d)
            op.deps.append(d)
        if not getattr(fn, "_is_nop", False):
            for d in op.deps:
                if d[0] == "d":
                    self.semres.setdefault(d[1], []).append(rref)
        return op

    def op(self, stream, fn, reads=(), writes=()):
        return self._add(stream, fn, list(reads), list(writes))

    def dma(self, stream, fn, sem, reads=(), writes=()):
        return self._add(stream, fn, list(reads), list(writes), dma=True, sem=sem)

    def barrier(self, streams=("pe", "act", "dve", "sp")):
        refs = []
        for s in streams:
            lst = self.streams[s]
            for o in reversed(lst):
                if not o.dma and not getattr(o.fn, "_is_nop", False):
                    refs.append(("c", s, o.idx))
                    break
        for sem, st in self.sem_stream.items():
            if st in streams:
                refs.append(("d", sem, self.dma_cum[sem]))
        def _nop(e):
            return e.nop()
        _nop._is_nop = True
        for s in streams:
            self._add(s, _nop, [], [], extra=[r for r in refs if not (r[0] == "c" and r[1] == s)])

    def emit(self):
        nc = self.nc

        def norm(d):
            if d[0] == "i":
                o = self.streams[d[1]][d[2]]
                return ("d", o.dmasem, o.dmaval)
            return d

        for s, lst in self.streams.items():
            for o in lst:
                nd = []
                for d in o.deps:
                    d = norm(d)
                    if d[0] == "c":
                        if d[1] == s:
                            if s in ("pe", "sp"):
                                continue
                            if d[2] >= o.idx:
                                continue
                        self.streams[d[1]][d[2]].marked = True
                    elif d[0] == "d":
                        if o.dma and d[1] == o.dmasem and d[2] >= o.dmaval:
                            continue
                    nd.append(d)
                o.deps = nd
        nsem = {}
        for s, lst in self.streams.items():
            cnt = 0
            for o in lst:
                if o.marked:
                    assert not o.dma and not getattr(o.fn, "_is_nop", False)
                    o.incsem = (s, cnt // EPOCH)
                    o.incval = cnt % EPOCH + 1
                    cnt += 1
            nsem[s] = (cnt + EPOCH - 1) // EPOCH
        with contextlib.ExitStack() as es:
            csem = {}
            for s, n in nsem.items():
                for e in range(n):
                    csem[(s, e)] = es.enter_context(nc.semaphore(f"c_{s}_{e}"))
            dsem = {}
            for name in self.dma_cum:
                dsem[name] = es.enter_context(nc.semaphore(f"d_{name}"))
            block = es.enter_context(nc.Block())

            def run_stream(s):
                def body(eng):
                    waited = {}
                    for o in self.streams[s]:
                        for d in o.deps:
                            if d[0] == "c":
                                po = self.streams[d[1]][d[2]]
                                key = ("c",) + po.incsem
                                val = po.incval
                                h = csem[po.incsem]
                                later = [k for k in waited if k[0] == "c" and k[1] == po.incsem[0] and k[2] > po.incsem[1]]
                                if later:
                                    continue
                            else:
                                key = ("d", d[1])
                                val = d[2]
                                h = dsem[d[1]]
                            if waited.get(key, 0) >= val:
                                continue
                            waited[key] = val
                            eng.wait_ge(h, val)
                        ins = o.fn(eng)
                        if o.dma:
                            ins.then_inc(dsem[o.dmasem], 16)
                        elif o.marked:
                            ins.then_inc(csem[o.incsem], 1)
                    if s == "sp":
                        for name, tot in self.dma_cum.items():
                            eng.wait_ge(dsem[name], tot)
                return body

            block.tensor(run_stream("pe"))
            block.scalar(run_stream("act"))
            block.vector(run_stream("dve"))
            block.gpsimd(run_stream("pool"))
            block.sync(run_stream("sp"))


D = 1024
T = 2048
CTX = 256
TT = T + CTX
L = 4
FH = 2816
NF = 22
PROJ_W = 7168
OFF_Q, OFF_K, OFF_V, OFF_GR, OFF_LX, OFF_LG, OFF_GA, OFF_GB = 0, 512, 1024, 2048, 3072, 4096, 5120, 6144
EPS = 1e-6
NT = [(0, 256, 0), (256, 512, 1), (768, 512, 1), (1280, 512, 1), (1792, 512, 1)]
NCH = 18
DBGH = 0

R_BMOD, R_NG, R_CW, R_CB, R_GB, R_LAM, R_FG, R_C, R_CC = 0, 36, 48, 64, 68, 84, 92, 93, 94

SB_BASE = 16512
SB_END = 229344
NSLOT = 4
SLOT_ELEMS = 2048


def build_program(n_layers=L, dbg=None):
    nc = bass.Bass("TRN2", target_bir_lowering=False)
    P = Prog(nc)

    def dram(name, shape, dt=F32, kind="ExternalInput"):
        return nc.dram_tensor(name, list(shape), dt, kind=kind).ap()

    x_d = dram("x", [T, D])
    ctx_d = dram("ctx", [CTX, D])
    vecs_d = dram("vecs", [128, D])
    lgraw_d = dram("lgraw", [128, 64])
    identf_d = dram("identf", [128, 128])
    relf_d = dram("relf", [128, 128])
    relb_d = dram("relb", [128, 128])
    idxq_d = dram("idxq", [128, 128])
    idxk_d = dram("idxk", [128, 16])
    rotc_d = dram("rotc", [128, TT], BF16)
    rots_d = dram("rots", [128, TT], BF16)
    w_mod_d = dram("w_mod", [L, D, 9 * D])
    wgu_d = [dram("ffn1_w_gu", [L, D, 2 * FH]), None, dram("ffn2_w_gu", [L, D, 2 * FH])]
    wdn_d = [dram("ffn1_w_down", [L, FH, D]), None, dram("ffn2_w_down", [L, FH, D])]
    w_in_d = dram("w_in", [L, D, PROJ_W])
    w_ret_o_d = dram("w_ret_o", [L, D, D])
    w_lru_o_d = dram("w_lru_o", [L, D, D])
    w_out_d = dram("w_out", [L, D, D])
    gate_w_d = dram("lru_gate_w", [L, 2, 2, 16, 64, 64])
    out_d = dram("out", [T, D], kind="ExternalOutput")
    dbg_d = dram("dbgx", [128, 8 * TT], kind="ExternalOutput") if dbg else None
    dbgr_d = dram("dbgr", [128, 6 * TT], BF16, kind="ExternalOutput") if dbg else None

    cnt = [0]
    cur = [SB_BASE]

    def salloc(name, shape, dt, at=None):
        nbytes = int(np.prod(shape[1:])) * (4 if dt == F32 else 2)
        nbytes = (nbytes + 31) // 32 * 32
        if at is None:
            off = cur[0]
            cur[0] += nbytes
        else:
            off = at[0]
            at[0] += nbytes
        assert off + nbytes <= SB_END, (name, off, nbytes)
        cnt[0] += 1
        return nc.alloc_sbuf_tensor_at(f"{name}_{cnt[0]}", list(shape), dt, offset=off)

    X = salloc("X", [128, 8, TT], F32)
    vecs = salloc("vecs", [128, 8, 96], F32)
    identf = salloc("identf", [128, 128], F32)
    identb = salloc("identb", [128, 128], BF16)
    onesb = salloc("onesb", [128, 128], BF16)
    relf = salloc("relf", [128, 128], F32)
    relb = salloc("relb", [128, 128], F32)
    idxq = salloc("idxq", [128, 128], F32)
    idxk = salloc("idxk", [128, 16], F32)
    lg = salloc("lg", [128, 64], F32)
    lgsel = salloc("lgsel", [128, 8], F32)
    csel = salloc("csel", [128, 8], F32)
    kdec = salloc("kdec", [128, 16], F32)
    lsc = salloc("lsc", [128, 8, 8], F32)
    modp = salloc("modp", [128, 2, 9, 8], F32)
    gsm = salloc("gsm", [128, 2, 3, 8], F32)
    hgm = salloc("hgm", [128, 2, 3, 8], F32)
    scb = salloc("scb", [128, 8, 2], BF16)
    gw = [salloc(f"gw{i}", [128, 4, 128], BF16) for i in range(2)]
    slots = [salloc(f"slot{i}", [128, SLOT_ELEMS], BF16) for i in range(NSLOT)]
    PH0 = cur[0]

    def phase_alloc():
        return [PH0]

    es = contextlib.ExitStack()
    NPS = 6
    ps = [es.enter_context(nc.psum_tensor(f"ps{i}", [128, 512], F32)) for i in range(NPS)]
    psb = es.enter_context(nc.psum_tensor("psb", [128, 1024], BF16))
    psn = [0]

    def nextps():
        i = psn[0] % NPS
        psn[0] += 1
        return ps[i], ("ps", i)

    def mm(out, lhsT, rhs, start, stop, reads, writes):
        P.op("pe", lambda e: e.matmul(out, lhsT=lhsT, rhs=rhs, start=start, stop=stop), reads, writes)

    def act(out, in_, func, reads, writes, bias=None, scale=None):
        kw = {}
        if bias is not None:
            kw["bias"] = bias
        if scale is not None:
            kw["scale"] = scale
        P.op("act", lambda e: e.activation(out=out, in_=in_, func=func, **kw), reads, writes)

    def tt(out, in0, in1, op, reads, writes):
        P.op("dve", lambda e: e.tensor_tensor(out=out, in0=in0, in1=in1, op=op), reads, writes)

    def ts(out, in0, s1, s2, op0, op1, reads, writes):
        if op1 is None:
            P.op("dve", lambda e: e.tensor_scalar(out=out, in0=in0, scalar1=s1, scalar2=None, op0=op0), reads, writes)
        else:
            P.op("dve", lambda e: e.tensor_scalar(out=out, in0=in0, scalar1=s1, scalar2=s2, op0=op0, op1=op1), reads, writes)

    def stt(out, in0, scalar, in1, op0, op1, reads, writes):
        P.op("dve", lambda e: e.scalar_tensor_tensor(out=out, in0=in0, scalar=scalar, in1=in1, op0=op0, op1=op1), reads, writes)

    def vcopy(out, in_, reads, writes):
        P.op("dve", lambda e: e.tensor_copy(out=out, in_=in_), reads, writes)

    def recip(out, in_, reads, writes):
        P.op("dve", lambda e: e.reciprocal(out=out, in_=in_), reads, writes)

    def memset(ap, val, writes):
        P.op("dve", lambda e: e.memset(ap, val), [], writes)

    def dma_sp(out, in_, sem, reads, writes):
        P.dma("sp", lambda e: e.dma_start(out=out, in_=in_), sem, reads, writes)

    def bmid(ap2, n):
        a = ap2.ap
        return bass.AP(ap2.tensor, ap2.offset, [list(a[0]), [0, n], list(a[1])])

    wcount = [0]

    def wload(parts):
        i = wcount[0]
        wcount[0] += 1
        s = i % NSLOT
        slot = slots[s]
        for dstf, src in parts:
            dst = dstf(slot)
            P.dma("pool", lambda e, dst=dst, src=src: e.dma_start(out=dst, in_=src), f"w{s}", [], [("slot", s)])
        return slot, ("slot", s)

    def kview(slot, k, n):
        return slot[:, 0:k * n].rearrange("p (k n) -> p k n", k=k)

    Xk = lambda dc, ti: ("X", dc, ti)

    dma_sp(identf[:], identf_d, "c0", [], ["identf"])
    dma_sp(relf[:], relf_d, "c0", [], ["relf"])
    dma_sp(relb[:], relb_d, "c0", [], ["relb"])
    dma_sp(idxq[:], idxq_d, "c0", [], ["idxq"])
    dma_sp(idxk[:], idxk_d, "c0", [], ["idxk"])
    dma_sp(lg[:], lgraw_d, "c0", [], ["lg"])
    vcopy(identb[:], identf[:], ["identf"], ["identb"])
    memset(onesb[:], 1.0, ["onesb"])
    memset(gw[0][:], 0.0, [("gw", 0)])
    memset(gw[1][:], 0.0, [("gw", 1)])

    ph = phase_alloc()
    stage = [salloc(f"stage{i}", [128, D], F32, at=ph) for i in range(4)]

    def load_T(src_rows, dst_fn, si, dkeys, ncol=128):
        st = stage[si % 4]
        dma_sp(st[:], src_rows, f"st{si % 4}", [], [("stage", si % 4)])
        for half in range(2):
            pt, pk = nextps()
            for j in range(4):
                dc = half * 4 + j
                P.op("pe", lambda e, pt=pt, j=j, dc=dc, st=st: e.transpose(out=pt[:, j * 128:(j + 1) * 128], in_=st[:, dc * 128:(dc + 1) * 128], identity=identf[:]),
                     [("stage", si % 4), "identf"], [pk])
            src = pt[:, :].rearrange("p (j n) -> p j n", j=4)[:, :, 0:ncol]
            dst = dst_fn(half)
            wk = dkeys(half)
            if (si + half) % 2 == 0:
                vcopy(dst, src, [pk], wk)
            else:
                act(dst, src, AF.Identity, [pk], wk)

    load_T(vecs_d, lambda h: vecs[:, 4 * h:4 * h + 4, :], 0, lambda h: ["vecs"], ncol=96)
    for blk in range(NCH):
        src = ctx_d[blk * 128:(blk + 1) * 128, :] if blk < 2 else x_d[(blk - 2) * 128:(blk - 1) * 128, :]
        ti = 0 if blk < 2 else 1 + (blk - 2) // 4
        load_T(src, lambda h, blk=blk: X[:, 4 * h:4 * h + 4, blk * 128:(blk + 1) * 128], blk + 1,
               lambda h, ti=ti: [Xk(dc, ti) for dc in range(4 * h, 4 * h + 4)])

    act(lg[:], lg[:], AF.Exp, ["lg"], ["lg"], scale=-1.0)
    act(lg[:], lg[:], AF.Ln, ["lg"], ["lg"], bias=1.0)
    ts(lg[:], lg[:], -1.0, None, ALU.mult, None, ["lg"], ["lg"])
    act(lsc[:], vecs[:, :, R_LAM:R_LAM + 8], AF.Exp, ["vecs"], ["lsc"], scale=-1.0)
    act(lsc[:], lsc[:], AF.Ln, ["lsc"], ["lsc"], bias=1.0)
    ts(lsc[:], lsc[:], -4.0, None, ALU.mult, None, ["lsc"], ["lsc"])
    ts(vecs[:, :, R_GB:R_GB + 16], vecs[:, :, R_GB:R_GB + 16], 0.5, None, ALU.mult, None, ["vecs"], ["vecs"])
    act(scb[:, :, 0], vecs[:, :, R_CC], AF.Silu, ["vecs"], ["scb"])
    act(scb[:, :, 1], vecs[:, :, R_C], AF.Silu, ["vecs"], ["scb"])

    def dump_dbg():
        dma_sp(dbg_d, X[:].rearrange("p k t -> p (k t)"), "dbg", [Xk(dc, ti) for dc in range(8) for ti in range(5)], ["dbg"])

    def norm_mod(tiles, k, hbuf, hcol0, hkey, sq, t32, rstd):
        for ti in tiles:
            t0, n, s = NT[ti]
            pss, pk = nextps()
            for kc in range(8):
                sqb = sq[kc % 2]
                act(sqb[:, :n], X[:, kc, t0:t0 + n], AF.Square, [Xk(kc, ti)], [("sq", kc % 2)])
                mm(pss[:, :n], onesb[:], sqb[:, :n], kc == 0, kc == 7, [("sq", kc % 2), "onesb"], [pk])
            ri = ti % len(rstd)
            rs = rstd[ri]
            act(rs[:, :n], pss[:, :n], AF.Ln, [pk], [("rstd", ri)], bias=EPS, scale=1.0 / D)
            act(rs[:, :n], rs[:, :n], AF.Exp, [("rstd", ri)], [("rstd", ri)], scale=-0.5)
            for kc in range(8):
                tb = t32[kc % 2]
                stt(tb[:, :n], X[:, kc, t0:t0 + n], gsm[:, s, k, kc:kc + 1], rs[:, :n], ALU.mult, ALU.mult,
                    [Xk(kc, ti), ("rstd", ri), "gsm"], [("t32", kc % 2)])
                if kc % 4 == 3:
                    ts(hbuf[:, kc, t0 - hcol0:t0 - hcol0 + n], tb[:, :n], modp[:, s, 3 * k, kc:kc + 1], None, ALU.add, None,
                       [("t32", kc % 2), "modp"], [(hkey, kc, ti)])
                else:
                    act(hbuf[:, kc, t0 - hcol0:t0 - hcol0 + n], tb[:, :n], AF.Identity, [("t32", kc % 2), "modp"], [(hkey, kc, ti)],
                        bias=modp[:, s, 3 * k, kc:kc + 1])

    pmod = es.enter_context(nc.psum_tensor("pmod", [128, 512], F32))
    mod_state = {"l": None, "next": 36}

    def mod_begin(l):
        mod_state["l"] = l
        mod_state["next"] = 0

    def mod_step(nsl=1, mslots=None):
        l = mod_state["l"]
        for _ in range(nsl):
            sl = mod_state["next"]
            if sl >= 36:
                return
            mod_state["next"] = sl + 1
            wv = w_mod_d[l].rearrange("(k p) n -> p k n", p=128)
            if mslots is None:
                slot, sk = wload([(lambda s: kview(s, 8, 256), wv[:, :, sl * 256:(sl + 1) * 256])])
            else:
                slot, sk = mslots[sl % 2], ("mslot", sl % 2)
                src = wv[:, :, sl * 256:(sl + 1) * 256]
                dst = kview(slot, 8, 256)
                P.dma("pool", lambda e, dst=dst, src=src: e.dma_start(out=dst, in_=src), f"m{sl % 2}", [], [sk])
            sv = kview(slot, 8, 256)
            for j in range(2):
                n = sl * 2 + j
                for kc in range(8):
                    mm(pmod[:, 2 * n:2 * n + 2], sv[:, kc, j * 128:(j + 1) * 128], scb[:, kc, :], kc == 0, kc == 7, [sk, "scb"], ["pmod"])

    def mod_finish(l):
        assert mod_state["l"] == l
        mod_step(36)
        pm, pmk = pmod, "pmod"
        for s in range(2):
            src = bass.AP(pm[:, :].tensor, pm[:, s:s + 1].offset, [list(pm[:, :].ap[0]), [16, 9], [2, 8]])
            bsrc = bass.AP(vecs[:].tensor, vecs[:, 0, R_BMOD + l * 9:R_BMOD + l * 9 + 1].offset, [list(vecs[:, 0, :].ap[0]), [1, 9], [96, 8]])
            tt(modp[:, s, :, :], src, bsrc, ALU.add, [pmk, "vecs"], ["modp"])
            for k in range(3):
                ng = vecs[:, :, R_NG + l * 3 + k]
                stt(gsm[:, s, k, :], modp[:, s, 3 * k + 1, :], 1.0, ng, ALU.add, ALU.mult, ["modp", "vecs"], ["gsm"])
                ts(hgm[:, s, k, :], modp[:, s, 3 * k + 2, :], (1.0 if k == 1 else 0.5), None, ALU.mult, None, ["modp"], ["hgm"])

    def ffn(l, k, skip_ctx):
        use_m = k == 2 and mod_state["next"] < 36
        P.barrier(("pe", "act", "dve", "sp", "pool") if use_m else ("pe", "act", "dve", "sp"))
        ph = phase_alloc()
        hTs = [salloc("hTg", [128, 8, 1280], BF16, at=ph), salloc("hTg", [128, 8, 1024], BF16, at=ph)]
        hid = salloc("hid", [128, NF, 1280], BF16, at=ph)
        sq = [salloc("sq", [128, 512], BF16, at=ph) for _ in range(2)]
        t32 = [salloc("t32", [128, 512], F32, at=ph) for _ in range(2)]
        rstd = [salloc("rstd", [128, 512], F32, at=ph) for _ in range(1)]
        sg = [salloc("sg", [128, 512], BF16, at=ph) for _ in range(2)]
        mslots = [salloc("mslot", [128, SLOT_ELEMS], BF16, at=ph) for _ in range(2)] if use_m else None
        wgu = wgu_d[k][l].rearrange("(k p) n -> p k n", p=128)
        wdn = wdn_d[k][l].rearrange("(f p) d -> p f d", p=128)
        groups = [[1, 2] if skip_ctx else [0, 1, 2], [3, 4]]
        sgc = 0
        norm_mod(groups[0], k, hTs[0], NT[groups[0][0]][0], "hg", sq, t32, rstd)
        for gi, g in enumerate(groups):
            g0 = NT[g[0]][0]
            hT = hTs[gi]
            for j in range(NF // 2):
                sA, kA = wload([(lambda s: kview(s, 8, 256), wgu[:, :, j * 256:(j + 1) * 256])])
                sB, kB = wload([(lambda s: kview(s, 8, 256), wgu[:, :, FH + j * 256:FH + (j + 1) * 256])])
                vA, vB = kview(sA, 8, 256), kview(sB, 8, 256)
                for fi in range(2):
                    f = 2 * j + fi
                    for ti in g:
                        t0, n, s = NT[ti]
                        c0 = t0 - g0
                        pa, pak = nextps()
                        pb, pbk = nextps()
                        for kc in range(8):
                            mm(pa[:, :n], vA[:, kc, fi * 128:(fi + 1) * 128], hT[:, kc, c0:c0 + n], kc == 0, kc == 7, [kA, ("hg", kc, ti)], [pak])
                        for kc in range(8):
                            mm(pb[:, :n], vB[:, kc, fi * 128:(fi + 1) * 128], hT[:, kc, c0:c0 + n], kc == 0, kc == 7, [kB, ("hg", kc, ti)], [pbk])
                        sgb = sg[sgc % 2]
                        act(sgb[:, :n], pa[:, :n], AF.Silu, [pak], [("sg", sgc % 2)])
                        tt(hid[:, f, c0:c0 + n], sgb[:, :n], pb[:, :n], ALU.mult, [("sg", sgc % 2), pbk], [("hid", f, ti)])
                        sgc += 1
                if use_m:
                    mod_step(2, mslots)
            if gi + 1 < len(groups):
                g1 = groups[gi + 1]
                norm_mod(g1, k, hTs[gi + 1], NT[g1[0]][0], "hg", sq, t32, rstd)
            for dc in range(8):
                sl = []
                for half in range(2):
                    sl.append(wload([(lambda s: kview(s, 11, 128), wdn[:, half * 11:(half + 1) * 11, dc * 128:(dc + 1) * 128])]))
                for ti in g:
                    t0, n, s = NT[ti]
                    c0 = t0 - g0
                    po, pok = nextps()
                    for f in range(NF):
                        slot, sk = sl[f // 11]
                        mm(po[:, :n], kview(slot, 11, 128)[:, f % 11, :], hid[:, f, c0:c0 + n], f == 0, f == NF - 1, [sk, ("hid", f, ti)], [pok])
                    stt(X[:, dc, t0:t0 + n], po[:, :n], hgm[:, s, k, dc:dc + 1], X[:, dc, t0:t0 + n], ALU.mult, ALU.add,
                        [pok, "hgm", Xk(dc, ti)], [Xk(dc, ti)])

    def mixer(l, last, only=None):
        P.barrier()
        ph = phase_alloc()
        hT = salloc("hT", [128, 8, TT], BF16, at=ph)
        OG = salloc("OG", [128, 8, TT], BF16, at=ph)
        TMP0 = ph[0]
        win = w_in_d[l].rearrange("(k p) n -> p k n", p=128)
        all_t = [0, 1, 2, 3, 4]
        out_t = [1, 2, 3, 4] if last else all_t

        pa = [TMP0]
        sq = [salloc("sq", [128, 512], BF16, at=pa) for _ in range(2)]
        t32 = [salloc("t32", [128, 512], F32, at=pa) for _ in range(2)]
        rstd = [salloc("rstd", [128, 512], F32, at=pa) for _ in range(2)]
        norm_mod(all_t, 1, hT, 0, "h", sq, t32, rstd)
        P.barrier()

        def hreads(ti):
            return [("h", kc, ti) for kc in range(8)]

        def final_stage(wbr_d, gate_off, pa):
            wbr = wbr_d[l].rearrange("(k p) n -> p k n", p=128)
            wo = w_out_d[l]
            Md = [salloc("Md", [128, TT], BF16, at=pa) for _ in range(2)]
            sgt = [salloc("sgt", [128, 512], BF16, at=pa) for _ in range(2)]
            c = 0
            for dc in range(8):
                slot, sk = wload([(lambda s: kview(s, 8, 256)[:, :, 0:128], wbr[:, :, dc * 128:(dc + 1) * 128]),
                                  (lambda s: kview(s, 8, 256)[:, :, 128:256], win[:, :, gate_off + dc * 128:gate_off + (dc + 1) * 128])])
                sv = kview(slot, 8, 256)
                so, sok = wload([(lambda s: s[:, 0:1024], wo[dc * 128:(dc + 1) * 128, :])])
                M = Md[dc % 2]
                for ti in out_t:
                    t0, n, s = NT[ti]
                    py, pyk = nextps()
                    pg, pgk = nextps()
                    for kc in range(8):
                        mm(py[:, :n], sv[:, kc, 0:128], OG[:, kc, t0:t0 + n], kc == 0, kc == 7, [sk, ("og", kc, ti)], [pyk])
                    for kc in range(8):
                        mm(pg[:, :n], sv[:, kc, 128:256], hT[:, kc, t0:t0 + n], kc == 0, kc == 7, [sk, ("h", kc, ti)], [pgk])
                    sb = sgt[c % 2]
                    act(sb[:, :n], pg[:, :n], AF.Sigmoid, [pgk], [("sgt", c % 2)])
                    tt(M[:, t0:t0 + n], sb[:, :n], py[:, :n], ALU.mult, [("sgt", c % 2), pyk], [("Md", dc % 2, ti)])
                    c += 1
                for ti in out_t:
                    t0, n, s = NT[ti]
                    for dc2 in range(8):
                        po, pok = nextps()
                        mm(po[:, :n], so[:, dc2 * 128:(dc2 + 1) * 128], M[:, t0:t0 + n], True, True, [sok, ("Md", dc % 2, ti)], [pok])
                        stt(X[:, dc2, t0:t0 + n], po[:, :n], hgm[:, s, 1, dc2:dc2 + 1], X[:, dc2, t0:t0 + n], ALU.mult, ALU.add,
                            [pok, "hgm", Xk(dc2, ti)], [Xk(dc2, ti)])

        def lru_branch():
            pa = [TMP0]
            UB = TT + 8
            ubuf = salloc("ubuf", [128, UB], BF16, at=pa)
            ucs = [salloc("uc", [128, TT], BF16, at=pa) for _ in range(2)]
            Afull = salloc("Afull", [128, TT], F32, at=pa)
            Iu = salloc("Iu", [128, TT], BF16, at=pa)
            S1 = salloc("S1", [128, TT], BF16, at=pa)
            Tt = [salloc("Tt", [128, 512], F32, at=pa)]
            off_t1 = pa[0]
            Tt.append(salloc("Tt", [128, 512], F32, at=pa))
            Qt = salloc("Qt", [128, 512], F32, at=pa)
            gt = salloc("gt", [128, 512], BF16, at=pa)
            gts = [(gt, "gt"), (salloc("gt2", [128, 512], BF16, at=[off_t1]), ("Qb", 1))]
            UOFF = [2, 261]
            memset(ubuf[:, 0:2], 0.0, ["ubufpad"])
            memset(ubuf[:, 258:261], 0.0, ["ubufpad"])
            memset(ubuf[:, 2309:UB], 0.0, ["ubufpad"])
            ubr = [("ubuf", ti) for ti in all_t] + ["ubufpad"]

            def load_slab(c):
                return wload([(lambda s: kview(s, 8, 256)[:, :, 0:128], win[:, :, OFF_LX + c * 128:OFF_LX + (c + 1) * 128]),
                              (lambda s: kview(s, 8, 256)[:, :, 128:256], win[:, :, OFF_LG + c * 128:OFF_LG + (c + 1) * 128])])

            def load_gw(c):
                gwt = gw[c % 2]
                for blk in range(2):
                    P.dma("pool", lambda e, blk=blk, gwt=gwt, c=c: e.dma_start(
                        out=gwt[blk * 64:(blk + 1) * 64, :, blk * 64:(blk + 1) * 64],
                        in_=gate_w_d[l, :, :, 2 * c + blk].rearrange("d g k j -> k (d g) j")),
                        f"gw{c % 2}", [], [("gw", c % 2)])

            def proj_u(c, slab):
                slot, sk = slab
                sv = kview(slot, 8, 256)
                for ti in all_t:
                    t0, n, s = NT[ti]
                    pu, puk = nextps()
                    for kc in range(8):
                        mm(pu[:, :n], sv[:, kc, 0:128], hT[:, kc, t0:t0 + n], kc == 0, kc == 7, [sk, ("h", kc, ti)], [puk])
                    uo = (UOFF[0] if ti == 0 else UOFF[1] - 256) + t0
                    act(ubuf[:, uo:uo + n], pu[:, :n], AF.Identity, [puk], [("ubuf", ti)])

            def conv_ops(c, ti):
                t0, n, s = NT[ti]
                uc = ucs[c % 2]
                base = (0 if ti == 0 else 259 - 256) + t0
                cw = lambda j: vecs[:, c, R_CW + l * 4 + j:R_CW + l * 4 + j + 1]
                ops = [lambda: ts(pmod[:, :n], ubuf[:, base:base + n], cw(0), vecs[:, c, R_CB + l:R_CB + l + 1], ALU.mult, ALU.add, ubr + ["vecs"], ["pmod"])]
                for j in (1, 2):
                    ops.append(lambda j=j: stt(pmod[:, :n], ubuf[:, base + j:base + j + n], cw(j), pmod[:, :n], ALU.mult, ALU.add, ubr + ["vecs", "pmod"], ["pmod"]))
                ops.append(lambda: stt(uc[:, t0:t0 + n], ubuf[:, base + 3:base + 3 + n], cw(3), pmod[:, :n], ALU.mult, ALU.add, ubr + ["vecs", "pmod"], [("uc", c % 2, ti)]))
                return ops

            def conv_tile(c, ti):
                for f_ in conv_ops(c, ti):
                    f_()

            slabs = {0: load_slab(0)}
            load_gw(0)
            slabs[1] = load_slab(1)
            proj_u(0, slabs[0])
            for ti in all_t:
                conv_tile(0, ti)
            tk = 0
            qc = [0]
            for c in range(8):
                slot, sk = slabs[c]
                sv = kview(slot, 8, 256)
                gwt = gw[c % 2]
                uc = ucs[c % 2]
                hfc = OG[:, c, :]
                if c + 1 < 8:
                    proj_u(c + 1, slabs[c + 1])
                Ak = lambda ti: ("A", ti)
                Ik = lambda ti: ("Iu", ti)
                Sk = lambda ti: ("S1", ti)
                allA = [Ak(ti) for ti in all_t]
                allI = [Ik(ti) for ti in all_t]
                allS = [Sk(ti) for ti in all_t]
                ogk = [("og", c, ti) for ti in all_t]
                cq = []
                if c + 1 < 8:
                    for ti in all_t:
                        cq += conv_ops(c + 1, ti)
                for d in range(2):
                    rb = R_GB + l * 4 + d * 2
                    hl = lsc[:, c, l * 2 + d:l * 2 + d + 1]
                    for ti in ((0, 4, 3, 2, 1) if d == 0 else all_t):
                        t0, n, s = NT[ti]
                        pr, prk = nextps()
                        pi, pik = nextps()
                        mm(pr[:, :n], gwt[:, d * 2 + 0, :], uc[:, t0:t0 + n], True, True, [("gw", c % 2), ("uc", c % 2, ti)], [prk])
                        mm(pi[:, :n], gwt[:, d * 2 + 1, :], uc[:, t0:t0 + n], True, True, [("gw", c % 2), ("uc", c % 2, ti)], [pik])
                        act(pr[:, :n], pr[:, :n], AF.Tanh, [prk, "vecs"], [prk], bias=vecs[:, c, rb:rb + 1], scale=0.5)
                        if d == 1:
                            act(hfc[:, t0:t0 + n], Afull[:, t0:t0 + n], AF.Identity, [Ak(ti)], [("og", c, ti)])
                        act(Afull[:, t0:t0 + n], pr[:, :n], AF.Exp, [prk, "lsc"], [Ak(ti)], bias=hl, scale=hl)
                        act(pi[:, :n], pi[:, :n], AF.Tanh, [pik, "vecs"], [pik], bias=vecs[:, c, rb + 1:rb + 2], scale=0.5)
                        stt(Iu[:, t0:t0 + n], pi[:, :n], 1.0, uc[:, t0:t0 + n], ALU.add, ALU.mult, [pik, ("uc", c % 2, ti)], [Ik(ti)])
                        qb = [Qt, Tt[1]][qc[0] % 2]
                        qk = ("Qb", qc[0] % 2)
                        qc[0] += 1
                        P.op("pool", lambda e, t0=t0, n=n, qb=qb: e.tensor_tensor(out=qb[:, :n], in0=Afull[:, t0:t0 + n], in1=Afull[:, t0:t0 + n], op=ALU.mult),
                             [Ak(ti)], [qk])
                        P.op("pool", lambda e, t0=t0, n=n, qb=qb: e.tensor_scalar(out=S1[:, t0:t0 + n], in0=qb[:, :n], scalar1=-1.0, scalar2=1.0, op0=ALU.mult, op1=ALU.add),
                             [qk], [Sk(ti)])
                        for _ in range(2):
                            if cq:
                                cq.pop(0)()
                    if d == 0:
                        if c + 2 < 8:
                            slabs[c + 2] = load_slab(c + 2)
                        if c + 1 < 8:
                            load_gw(c + 1)
                    act(S1[:, :], S1[:, :], AF.Sqrt, allS, allS, scale=0.25)
                    tt(Iu[:, :], Iu[:, :], S1[:, :], ALU.mult, allI + allS, allI)
                    if d == 0:
                        for ti in all_t:
                            t0, n, s = NT[ti]
                            init = 0.0 if ti == 0 else Afull[:, t0 - 1:t0]
                            rd = [Ak(ti), Ik(ti)] + ([Ak(ti - 1)] if ti else [])
                            P.op("dve", lambda e, t0=t0, n=n, init=init: e.tensor_tensor_scan(out=Afull[:, t0:t0 + n], data0=Afull[:, t0:t0 + n], data1=Iu[:, t0:t0 + n],
                                                                                           initial=init, op0=ALU.mult, op1=ALU.add), rd, [Ak(ti)])
                    else:
                        def out_tile(ti, oi):
                            t0, n, s = NT[ti]
                            gb, gk = gts[oi % 2]
                            pg, pgk = nextps()
                            for kc in range(8):
                                mm(pg[:, :n], sv[:, kc, 128:256], hT[:, kc, t0:t0 + n], kc == 0, kc == 7, [sk, ("h", kc, ti)], [pgk])
                            act(gb[:, :n], pg[:, :n], AF.Gelu_apprx_tanh, [pgk], [gk])
                            tt(Tt[0][:, :n], hfc[:, t0:t0 + n], Afull[:, t0:t0 + n], ALU.add, [("og", c, ti), Ak(ti)], [("Tt", 0)])
                            tt(OG[:, c, t0:t0 + n], Tt[0][:, :n], gb[:, :n], ALU.mult, [("Tt", 0), gk], [("og", c, ti)])

                        while cq:
                            cq.pop(0)()
                        prev = None
                        order = (0, 4, 3, 2, 1)
                        for oi, ti in enumerate(order):
                            t0, n, s = NT[ti]
                            if prev is None:
                                init = 0.0
                            else:
                                p0 = NT[prev][0]
                                init = Afull[:, p0:p0 + 1]
                            rd = [Ak(ti), Ik(ti)] + ([Ak(prev)] if prev is not None else [])
                            P.op("dve", lambda e, t0=t0, n=n, init=init: e.tensor_tensor_scan(out=Afull[:, t0:t0 + n][:, ::-1], data0=Afull[:, t0:t0 + n][:, ::-1],
                                                                                           data1=Iu[:, t0:t0 + n][:, ::-1], initial=init, op0=ALU.mult, op1=ALU.add),
                                 rd, [Ak(ti)])
                            if prev is not None and not (last and prev == 0):
                                out_tile(prev, oi - 1)
                            prev = ti
                        out_tile(prev, len(order) - 1)
            P.barrier()
            pa = [TMP0]
            final_stage(w_lru_o_d, OFF_GB, pa)
            P.barrier()

        def ret_branch():
            pa = [TMP0]
            qd = salloc("qd", [128, TT], BF16, at=pa)
            kT = salloc("kT", [128, TT], BF16, at=pa)
            vtok = salloc("vtok", [128, NCH, 128], BF16, at=pa)
            kd = salloc("kd", [128, NCH, 128], BF16, at=pa)
            Scat = salloc("Scat", [128, NCH, 128], BF16, at=pa)
            rc = [salloc("rc", [128, 512], BF16, at=pa) for _ in range(2)]
            rs_ = [salloc("rs", [128, 512], BF16, at=pa) for _ in range(2)]
            t1 = salloc("t1", [128, 512], F32, at=pa)
            t2 = salloc("t2", [128, 512], F32, at=pa)
            DT = salloc("DT", [128, 128], F32, at=pa)
            dtab = salloc("dtab", [128, 128], BF16, at=pa)
            PT = [salloc("PT", [128, 512], BF16, at=pa) for _ in range(2)]
            qdec = [salloc("qdec", [128, 512], BF16, at=pa) for _ in range(2)]
            Sst = [salloc("Sst", [128, 128], F32, at=pa) for _ in range(2)]
            ob, o2, sgr = rc[0], rs_[0], rc[1]
            OBK, O2K, SGK = ("rc", 0), ("rs", 0), ("rc", 1)
            mean, var = t1, t2
            cen = salloc("cen", [128, 512], F32, at=pa)

            vcopy(lgsel[0:64, :], lg[0:64, l * 16:l * 16 + 8], ["lg"], ["lgsel"])
            vcopy(lgsel[64:128, :], lg[64:128, l * 16 + 8:l * 16 + 16], ["lg"], ["lgsel"])
            act(csel[:], lgsel[:], AF.Exp, ["lgsel"], ["csel"], scale=128.0)
            tt(kdec[:], lg[:, l * 16:(l + 1) * 16], idxk[:], ALU.mult, ["lg", "idxk"], ["kdec"])
            act(kdec[:], kdec[:], AF.Exp, ["kdec"], ["kdec"])

            rcount = [0]

            def proj_rot(slot_v, sk, ti, dst, dkey, kscale):
                t0, n, s = NT[ti]
                ri = rcount[0] % 2
                rcount[0] += 1
                dma_sp(rc[ri][:, :n], rotc_d[:, t0:t0 + n], f"rc{ri}", [], [("rc", ri)])
                dma_sp(rs_[ri][:, :n], rots_d[:, t0:t0 + n], f"rs{ri}", [], [("rs", ri)])
                p1, p1k = nextps()
                p2, p2k = nextps()
                for kc in range(8):
                    mm(p1[:, :n], slot_v[:, kc, 0:128], hT[:, kc, t0:t0 + n], kc == 0, kc == 7, [sk, ("h", kc, ti)], [p1k])
                for kc in range(8):
                    mm(p2[:, :n], slot_v[:, kc, 128:256], hT[:, kc, t0:t0 + n], kc == 0, kc == 7, [sk, ("h", kc, ti)], [p2k])
                stt(t1[:, :n], p1[:, :n], kscale, rc[ri][:, :n], ALU.mult, ALU.mult, [p1k, ("rc", ri)], ["t1"])
                stt(t2[:, :n], p2[:, :n], kscale, rs_[ri][:, :n], ALU.mult, ALU.mult, [p2k, ("rs", ri)], ["t2"])
                tt(dst[:, t0:t0 + n], t1[:, :n], t2[:, :n], ALU.add, ["t1", "t2"], [(dkey, ti)])

            def dup_cols(s, off, c0, w):
                return [(lambda sl, o=off + r * w: kview(sl, 8, 256)[:, :, o:o + w], win[:, :, c0:c0 + w]) for r in range(2)]

            for h in range(8):
                r0 = 0 if h % 2 == 0 else 64
                qc = OFF_Q + h * 64
                parts = []
                parts += [(lambda sl, o=r * 64: kview(sl, 8, 256)[:, :, o:o + 64], win[:, :, qc:qc + 64]) for r in range(2)]
                for r in range(2):
                    parts.append((lambda sl, o=128 + r * 64: kview(sl, 8, 256)[:, :, o:o + 32], win[:, :, qc + 32:qc + 64]))
                    parts.append((lambda sl, o=128 + r * 64 + 32: kview(sl, 8, 256)[:, :, o:o + 32], win[:, :, qc:qc + 32]))
                sq_, sqk = wload(parts)
                if h % 2 == 0:
                    kc_ = OFF_K + h * 64
                    parts = [(lambda sl: kview(sl, 8, 256)[:, :, 0:128], win[:, :, kc_:kc_ + 128])]
                    for r in range(2):
                        parts.append((lambda sl, o=128 + r * 64: kview(sl, 8, 256)[:, :, o:o + 32], win[:, :, kc_ + r * 64 + 32:kc_ + r * 64 + 64]))
                        parts.append((lambda sl, o=128 + r * 64 + 32: kview(sl, 8, 256)[:, :, o:o + 32], win[:, :, kc_ + r * 64:kc_ + r * 64 + 32]))
                    sk_, skk = wload(parts)
                sv_, svk = wload([(lambda sl: kview(sl, 8, 256)[:, :, 0:128], win[:, :, OFF_V + h * 128:OFF_V + (h + 1) * 128]),
                                  (lambda sl: kview(sl, 8, 256)[:, :, 128:256], win[:, :, OFF_GR + h * 128:OFF_GR + (h + 1) * 128])])
                svv = kview(sv_, 8, 256)
                for ti in all_t:
                    proj_rot(kview(sq_, 8, 256), sqk, ti, qd, "qd", 1.0)
                    if h % 2 == 0:
                        proj_rot(kview(sk_, 8, 256), skk, ti, kT, "kT", 0.125)
                for ti in all_t:
                    t0, n, s = NT[ti]
                    pv, pvk = nextps()
                    nch = n // 128
                    for cc in range(nch):
                        ch = t0 // 128 + cc
                        for kc in range(8):
                            mm(pv[:, cc * 128:(cc + 1) * 128], hT[:, kc, ch * 128:(ch + 1) * 128], svv[:, kc, 0:128], kc == 0, kc == 7,
                               [("h", kc, ti), svk], [pvk])
                    act(vtok[:, t0 // 128:t0 // 128 + nch, :], pv[:, :n].rearrange("p (c v) -> p c v", c=nch), AF.Identity, [pvk], [("vtok", ti)])
                for (c0, c1) in ((0, 8), (8, 16), (16, 18)):
                    tis = sorted(set((0 if ch < 2 else 1 + (ch - 2) // 4) for ch in range(c0, c1)))
                    for ch in range(c0, c1):
                        P.op("pe", lambda e, ch=ch, c0=c0: e.transpose(out=psb[:, (ch - c0) * 128:(ch - c0 + 1) * 128], in_=kT[:, ch * 128:(ch + 1) * 128],
                                                                       identity=identb[:]),
                             [("kT", ti) for ti in tis] + ["identb"], ["psb"])
                    src = psb[:, 0:(c1 - c0) * 128].rearrange("p (c d) -> p c d", d=128)[:, :, r0:r0 + 64]
                    ts(kd[:, c0:c1, 0:64], src, kdec[:, h:h + 1], None, ALU.mult, None, ["psb", "kdec"], [("kd", c0)])
                    ts(kd[:, c0:c1, 64:128], src, kdec[:, 8 + h:9 + h], None, ALU.mult, None, ["psb", "kdec"], [("kd", c0)])
                kdr = [("kd", 0), ("kd", 8), ("kd", 16)]
                vr = [("vtok", ti) for ti in all_t]
                ts(DT[:], relf[:], lg[:, l * 16 + h:l * 16 + h + 1], None, ALU.mult, None, ["relf", "lg"], ["DT"])
                stt(DT[:], relb[:], lg[:, l * 16 + 8 + h:l * 16 + 9 + h], DT[:], ALU.mult, ALU.add, ["relb", "lg", "DT"], ["DT"])
                act(DT[:], DT[:], AF.Exp, ["DT"], ["DT"])
                act(dtab[:], idxq[:], AF.Exp, ["idxq", "lgsel"], ["dtab"], scale=lgsel[:, h:h + 1])
                kvp = []
                for b in range(5):
                    pk_, pkk = nextps()
                    kvp.append((pk_, pkk))
                for ch in range(NCH):
                    pk_, pkk = kvp[ch // 4]
                    mm(pk_[:, (ch % 4) * 128:(ch % 4 + 1) * 128], kd[:, ch, :], vtok[:, ch, :], True, True, kdr + vr, [pkk])

                def kvs(rows, ch):
                    pk_, pkk = kvp[ch // 4]
                    return pk_[rows, (ch % 4) * 128:(ch % 4 + 1) * 128], pkk

                fr = slice(0, 64)
                br = slice(64, 128)
                memset(Sst[0][:, :], 0.0, [("Sf", 0), ("Sb", 0)])
                memset(Scat[fr, 0, :], 0.0, [("ScatF", 0)])
                memset(Scat[br, 1, :], 0.0, [("ScatB", 1)])
                seq = [(1, 0), (0, 17)] + [(n_, n_ - 1) for n_ in range(17, 2, -1)]
                for j in range(NCH - 1):
                    cur, nx = j % 2, (j + 1) % 2
                    kv, kvk = kvs(fr, j)
                    stt(Sst[nx][fr, :], Sst[cur][fr, :], csel[fr, h:h + 1], kv, ALU.mult, ALU.add, [("Sf", cur), "csel", kvk], [("Sf", nx)])
                    src_ch, dst_ch = seq[j]
                    kv, kvk = kvs(br, src_ch)
                    stt(Sst[nx][br, :], Sst[cur][br, :], csel[br, h:h + 1], kv, ALU.mult, ALU.add, [("Sb", cur), "csel", kvk], [("Sb", nx)])
                    act(Scat[fr, j + 1, :], Sst[nx][fr, :], AF.Identity, [("Sf", nx)], [("ScatF", j + 1)])
                    act(Scat[br, dst_ch, :], Sst[nx][br, :], AF.Identity, [("Sb", nx)], [("ScatB", dst_ch)])
                for ti in out_t:
                    t0, n, s = NT[ti]
                    nch = n // 128
                    cb = t0 // 128
                    pi_ = ti % 2
                    psx, psk = nextps()
                    for cc in range(nch):
                        ch = cb + cc
                        mm(psx[:, cc * 128:(cc + 1) * 128], kT[r0:r0 + 64, ch * 128:(ch + 1) * 128], qd[r0:r0 + 64, ch * 128:(ch + 1) * 128], True, True,
                           [("kT", ti), ("qd", ti)], [psk])
                    tt(PT[pi_][:, :n].rearrange("p (c i) -> p c i", c=nch), psx[:, :n].rearrange("p (c i) -> p c i", c=nch), bmid(DT[:, :], nch), ALU.mult,
                       [psk, "DT"], [("PT", pi_)])
                    tt(qdec[pi_][:, :n].rearrange("p (c i) -> p c i", c=nch), qd[:, t0:t0 + n].rearrange("p (c i) -> p c i", c=nch), bmid(dtab[:, :], nch), ALU.mult,
                       [("qd", ti), "dtab"], [("qdec", pi_)])
                    po, pok = nextps()
                    for cc in range(nch):
                        ch = cb + cc
                        mm(po[:, cc * 128:(cc + 1) * 128], vtok[:, ch, :], PT[pi_][:, cc * 128:(cc + 1) * 128], True, False, vr + [("PT", pi_)], [pok])
                        mm(po[:, cc * 128:(cc + 1) * 128], Scat[:, ch, :], qdec[pi_][:, cc * 128:(cc + 1) * 128], False, True,
                           [("ScatF", ch), ("ScatB", ch), ("qdec", pi_)], [pok])
                    act(ob[:, :n], po[:, :n], AF.Identity, [pok], [OBK])
                    act(o2[:, :n], po[:, :n], AF.Square, [pok], [O2K])
                    pm1, pm1k = nextps()
                    pm2, pm2k = nextps()
                    mm(pm1[:, :n], onesb[:], ob[:, :n], True, True, [OBK, "onesb"], [pm1k])
                    mm(pm2[:, :n], onesb[:], o2[:, :n], True, True, [O2K, "onesb"], [pm2k])
                    ts(mean[:, :n], pm1[:, :n], 1.0 / 128, None, ALU.mult, None, [pm1k], ["t1"])
                    tt(var[:, :n], mean[:, :n], mean[:, :n], ALU.mult, ["t1"], ["t2"])
                    stt(var[:, :n], pm2[:, :n], 1.0 / 128, var[:, :n], ALU.mult, ALU.subtract, [pm2k, "t2"], ["t2"])
                    ts(var[:, :n], var[:, :n], 0.0, EPS, ALU.max, ALU.add, ["t2"], ["t2"])
                    act(var[:, :n], var[:, :n], AF.Ln, ["t2"], ["t2"])
                    act(var[:, :n], var[:, :n], AF.Exp, ["t2"], ["t2"], scale=-0.5)
                    tt(cen[:, :n], po[:, :n], mean[:, :n], ALU.subtract, [pok, "t1"], ["cen"])
                    tt(cen[:, :n], cen[:, :n], var[:, :n], ALU.mult, ["cen", "t2"], ["cen"])
                    pg, pgk = nextps()
                    for kc in range(8):
                        mm(pg[:, :n], svv[:, kc, 128:256], hT[:, kc, t0:t0 + n], kc == 0, kc == 7, [svk, ("h", kc, ti)], [pgk])
                    act(sgr[:, :n], pg[:, :n], AF.Silu, [pgk], [SGK])
                    tt(OG[:, h, t0:t0 + n], cen[:, :n], sgr[:, :n], ALU.mult, ["cen", SGK], [("og", h, ti)])
                if dbg and h == DBGH:
                    P.barrier()
                    for i_, tsr in enumerate((qd[:, :], kT[:, :], vtok[:].rearrange("p c v -> p (c v)"), kd[:].rearrange("p c v -> p (c v)"),
                                              Scat[:].rearrange("p c v -> p (c v)"), OG[:, h, :])):
                        dma_sp(dbgr_d[:, i_ * TT:(i_ + 1) * TT], tsr, "dbgr", [], [("dbgr", i_)])
                    P.barrier()
            P.barrier()
            pa = [TMP0]
            final_stage(w_ret_o_d, OFF_GA, pa)

        if only != 'ret':
            lru_branch()
        if only != 'lru':
            ret_branch()

    def final_out():
        P.barrier()
        ph = phase_alloc()
        sq = [salloc("sq", [128, 512], BF16, at=ph) for _ in range(2)]
        rstd = [salloc("rstd", [128, 512], F32, at=ph) for _ in range(2)]
        y = [salloc("y", [128, 8, 512], F32, at=ph) for _ in range(2)]
        ost = [salloc("ost", [128, D], F32, at=ph) for _ in range(2)]
        oc = 0
        for ti in (1, 2, 3, 4):
            t0, n, s = NT[ti]
            pss, pk = nextps()
            for kc in range(8):
                sqb = sq[kc % 2]
                act(sqb[:, :n], X[:, kc, t0:t0 + n], AF.Square, [Xk(kc, ti)], [("sq", kc % 2)])
                mm(pss[:, :n], onesb[:], sqb[:, :n], kc == 0, kc == 7, [("sq", kc % 2), "onesb"], [pk])
            rs = rstd[ti % 2]
            act(rs[:, :n], pss[:, :n], AF.Ln, [pk], [("rstd", ti % 2)], bias=EPS, scale=1.0 / D)
            act(rs[:, :n], rs[:, :n], AF.Exp, [("rstd", ti % 2)], [("rstd", ti % 2)], scale=-0.5)
            yb = y[ti % 2]
            for kc in range(8):
                stt(yb[:, kc, :n], X[:, kc, t0:t0 + n], vecs[:, kc, R_FG:R_FG + 1], rs[:, :n], ALU.mult, ALU.mult,
                    [Xk(kc, ti), ("rstd", ti % 2), "vecs"], [("y", ti % 2, kc)])
            for cc in range(n // 128):
                ob_ = ost[oc % 2]
                for half in range(2):
                    pt, ptk = nextps()
                    for j in range(4):
                        kc = half * 4 + j
                        P.op("pe", lambda e, pt=pt, j=j, kc=kc, yb=yb, cc=cc: e.transpose(out=pt[:, j * 128:(j + 1) * 128], in_=yb[:, kc, cc * 128:(cc + 1) * 128], identity=identf[:]),
                             [("y", ti % 2, kc), "identf"], [ptk])
                    if half == 0:
                        vcopy(ob_[:, 0:512], pt[:, :], [ptk], [("ost", oc % 2)])
                    else:
                        act(ob_[:, 512:1024], pt[:, :], AF.Identity, [ptk], [("ost", oc % 2)])
                row0 = t0 - CTX + cc * 128
                dma_sp(out_d[row0:row0 + 128, :], ob_[:], f"o{oc % 2}", [("ost", oc % 2)], [("out", oc)])
                oc += 1

    done = False
    for l in range(n_layers):
        last = l == L - 1
        if l == 0:
            mod_begin(0)
        mod_finish(l)
        ffn(l, 0, False)
        if dbg == (l, "ffn1"):
            done = True
            break
        if dbg and dbg[0] == l and dbg[1] in ("mixer_lru", "mixer_ret"):
            mixer(l, last, only=dbg[1][6:])
            done = True
            break
        mixer(l, last)
        if dbg == (l, "mixer"):
            done = True
            break
        if l + 1 < n_layers:
            mod_begin(l + 1)
        ffn(l, 2, last)
        if dbg == (l, "ffn2"):
            done = True
            break
    if dbg:
        P.barrier()
        dump_dbg()
    final_out()
    P.emit()
    es.close()
    return nc


def _consts():
    idx = np.arange(128, dtype=np.float32)
    rel = idx[None, :] - idx[:, None]
    relf = np.maximum(rel, 0.0).astype(np.float32)
    relb = np.maximum(-rel, 0.0).astype(np.float32)
    idxq = np.empty((128, 128), np.float32)
    idxq[:64, :] = idx[None, :] + 1.0
    idxq[64:, :] = 128.0 - idx[None, :]
    idxk = np.empty((128, 16), np.float32)
    idxk[:, :8] = (127.0 - idx)[:, None]
    idxk[:, 8:] = idx[:, None]
    rows = T // 64
    row = np.repeat(np.arange(rows, dtype=np.float32), 64)
    col = np.tile(np.arange(64, dtype=np.float32), rows)
    n_f = 16
    inv = (10000.0 ** (-np.arange(n_f, dtype=np.float32) / n_f)).astype(np.float32)
    ang = np.concatenate([row[:, None] * inv, col[:, None] * inv], axis=-1)
    cos, sin = np.cos(ang).astype(np.float32), np.sin(ang).astype(np.float32)
    C64 = np.concatenate([cos, cos], axis=1).T
    S64 = np.concatenate([-sin, sin], axis=1).T
    rotc = np.ones((128, TT), np.float32)
    rots = np.zeros((128, TT), np.float32)
    rotc[:64, CTX:] = C64
    rotc[64:, CTX:] = C64
    rots[:64, CTX:] = S64
    rots[64:, CTX:] = S64
    return dict(identf=np.eye(128, dtype=np.float32), relf=relf, relb=relb, idxq=idxq, idxk=idxk,
                rotc=rotc.astype(ml_dtypes.bfloat16), rots=rots.astype(ml_dtypes.bfloat16))


_NC_CACHE = {}


def _prep_inputs(inp):
    f = lambda a: np.ascontiguousarray(np.asarray(a, dtype=np.float32))
    consts = _consts()
    shared = {k: f(inp[k]) for k in ("w_mod", "ffn1_w_gu", "ffn1_w_down", "ffn2_w_gu", "ffn2_w_down", "w_in",
                                      "w_ret_o", "w_lru_o", "w_out", "lru_gate_w")}
    shared.update(consts)
    shared["lgraw"] = np.ascontiguousarray(np.broadcast_to(f(inp["ret_decay_logit"]).reshape(1, 64), (128, 64)))
    base = np.zeros((128, D), np.float32)
    base[R_BMOD:R_BMOD + 36] = f(inp["b_mod"]).reshape(36, D)
    base[R_NG:R_NG + 12] = f(inp["norm_g"]).reshape(12, D)
    base[R_CW:R_CW + 16] = f(inp["lru_conv_w"]).reshape(16, D)
    base[R_CB:R_CB + 4] = f(inp["lru_conv_b"]).reshape(4, D)
    base[R_GB:R_GB + 16] = f(inp["lru_gate_b"]).reshape(16, D)
    base[R_LAM:R_LAM + 8] = f(inp["lru_lambda"]).reshape(8, D)
    base[R_FG] = f(inp["final_g"])
    base[R_CC] = f(inp["c_ctx"])
    x = f(inp["x"])
    ctx = f(inp["ctx"])
    c = f(inp["c"])
    maps = []
    for b in range(8):
        v = base.copy()
        v[R_C] = c[b]
        m = dict(shared)
        m["x"] = x[b]
        m["ctx"] = ctx[b]
        m["vecs"] = v
        maps.append(m)
    return maps


def kernel(**inputs):
    if "nc" not in _NC_CACHE:
        _NC_CACHE["nc"] = build_program()
    nc = _NC_CACHE["nc"]
    maps = _prep_inputs(inputs)
    res = run_bass_kernel_spmd(nc, maps, core_ids=list(range(8)))
    return np.stack([np.asarray(r["out"], dtype=np.float32) for r in res.results], axis=0)
```
